# Optimizing a Trainium2 kernel written in Bass

```python
import jax, jax.numpy as jnp
from jax import lax
import numpy as np

D_MODEL = 1024
BATCH = 8
SEQ = 4096
DEPTH = 4

CHUNK = 64
N_A = DEPTH // 2
N_B = DEPTH - N_A
GMLP_BLOCK = 128
GMLP_WIDTH = D_MODEL
GMLP_GROUPS = 8
GMLP_GROUP_DIM = GMLP_WIDTH // GMLP_GROUPS
SB_HEADS = 16
SB_HEAD_DIM = D_MODEL // SB_HEADS
SB_QBLOCK = 128
D_FF = 4 * D_MODEL
ALPHA = float((2 * DEPTH) ** 0.25)
BETA = float((8 * DEPTH) ** -0.25)
LN_EPS = 1e-5

kernel_name = "yoco_gmlp_stickbreaking_deepnorm"


def layer_norm(x, g, b):
    xf = x.astype(jnp.float32)
    mu = jnp.mean(xf, axis=-1, keepdims=True)
    var = jnp.mean(jnp.square(xf - mu), axis=-1, keepdims=True)
    y = (xf - mu) * lax.rsqrt(var + LN_EPS) * g.astype(jnp.float32) + b.astype(jnp.float32)
    return y.astype(x.dtype)


def chunk_causal_mask(n):
    pos = jnp.arange(n)
    return (pos[None, :] // CHUNK) <= (pos[:, None] // CHUNK)


def gmlp_mixer(x, w_in, ln_g, ln_b, w_s, b_s, w_out):
    bsz, seq, _ = x.shape
    z = jax.nn.gelu(x @ w_in)
    u, v = jnp.split(z, 2, axis=-1)
    v = layer_norm(v, ln_g, ln_b)
    nblk = seq // GMLP_BLOCK
    v = v.reshape(bsz, nblk, GMLP_BLOCK, GMLP_GROUPS, GMLP_GROUP_DIM)
    ws = jnp.where(chunk_causal_mask(GMLP_BLOCK)[None], w_s, jnp.zeros((), w_s.dtype))
    s = jnp.einsum('gts,bnsgc->bntgc', ws, v)
    s = s + jnp.transpose(b_s)[None, None, :, :, None]
    s = s.reshape(bsz, seq, GMLP_WIDTH)
    return (u * s) @ w_out


def stick_breaking_mixer(x, w_q, w_o, k, v):
    bsz, seq, _ = x.shape
    q = (x @ w_q).reshape(bsz, seq, SB_HEADS, SB_HEAD_DIM).transpose(0, 2, 1, 3)
    scale = SB_HEAD_DIM ** -0.5
    outs = []
    for i in range(seq // SB_QBLOCK):
        q0, q1 = i * SB_QBLOCK, (i + 1) * SB_QBLOCK
        qb = q[:, :, q0:q1]
        kb = k[:, :, :q1]
        vb = v[:, :, :q1]
        zlog = jnp.einsum('bhtd,bhsd->bhts', qb, kb).astype(jnp.float32) * scale
        t_idx = q0 + jnp.arange(SB_QBLOCK)
        s_idx = jnp.arange(q1)
        causal = s_idx[None, :] < t_idx[:, None]
        log_rem = jnp.where(causal, jax.nn.log_sigmoid(-zlog), 0.0)
        excl = lax.cumsum(log_rem, axis=3, reverse=True) - log_rem
        log_a = jax.nn.log_sigmoid(zlog) + excl
        a = jnp.where(causal, jnp.exp(log_a), 0.0).astype(vb.dtype)
        outs.append(jnp.einsum('bhts,bhsd->bhtd', a, vb))
    o = jnp.concatenate(outs, axis=2)
    o = o.transpose(0, 2, 1, 3).reshape(bsz, seq, D_MODEL)
    return o @ w_o


def squared_relu_mlp(x, w1, w2):
    return jnp.square(jax.nn.relu(x @ w1)) @ w2


def setup_inputs(seed: int = 0) -> dict:
    key = jax.random.key(seed)
    ks = jax.random.split(key, 20)

    def nrm(k, shape, scale):
        return jax.random.normal(k, shape, jnp.float32) * scale

    return {
        "x": nrm(ks[0], (BATCH, SEQ, D_MODEL), 1.0),
        "a_w_in": nrm(ks[1], (N_A, D_MODEL, 2 * GMLP_WIDTH), D_MODEL ** -0.5),
        "a_ln_g": 1.0 + nrm(ks[2], (N_A, GMLP_WIDTH), 0.02),
        "a_ln_b": nrm(ks[3], (N_A, GMLP_WIDTH), 0.02),
        "a_w_s": nrm(ks[4], (N_A, GMLP_GROUPS, GMLP_BLOCK, GMLP_BLOCK), GMLP_BLOCK ** -0.5),
        "a_b_s": 1.0 + nrm(ks[5], (N_A, GMLP_GROUPS, GMLP_BLOCK), 0.02),
        "a_w_out": nrm(ks[6], (N_A, GMLP_WIDTH, D_MODEL), BETA * GMLP_WIDTH ** -0.5),
        "sb_w_k": nrm(ks[7], (D_MODEL, D_MODEL), D_MODEL ** -0.5),
        "sb_w_v": nrm(ks[8], (D_MODEL, D_MODEL), BETA * D_MODEL ** -0.5),
        "b_w_q": nrm(ks[9], (N_B, D_MODEL, D_MODEL), D_MODEL ** -0.5),
        "b_w_o": nrm(ks[10], (N_B, D_MODEL, D_MODEL), BETA * D_MODEL ** -0.5),
        "mix_ln_g": 1.0 + nrm(ks[11], (DEPTH, D_MODEL), 0.02),
        "mix_ln_b": nrm(ks[12], (DEPTH, D_MODEL), 0.02),
        "ffn_ln_g": 1.0 + nrm(ks[13], (DEPTH, D_MODEL), 0.02),
        "ffn_ln_b": nrm(ks[14], (DEPTH, D_MODEL), 0.02),
        "ffn_w1": nrm(ks[15], (DEPTH, D_MODEL, D_FF), BETA * D_MODEL ** -0.5),
        "ffn_w2": nrm(ks[16], (DEPTH, D_FF, D_MODEL), BETA * D_FF ** -0.5),
    }


def reference(x, a_w_in, a_ln_g, a_ln_b, a_w_s, a_b_s, a_w_out, sb_w_k, sb_w_v,
              b_w_q, b_w_o, mix_ln_g, mix_ln_b, ffn_ln_g, ffn_ln_b, ffn_w1, ffn_w2):
    bsz, seq, _ = x.shape
    k_shared = None
    v_shared = None
    for l in range(DEPTH):
        if l < N_A:
            mix = gmlp_mixer(x, a_w_in[l], a_ln_g[l], a_ln_b[l], a_w_s[l], a_b_s[l], a_w_out[l])
        else:
            if l == N_A:
                k_shared = (x @ sb_w_k).reshape(bsz, seq, SB_HEADS, SB_HEAD_DIM).transpose(0, 2, 1, 3)
                v_shared = (x @ sb_w_v).reshape(bsz, seq, SB_HEADS, SB_HEAD_DIM).transpose(0, 2, 1, 3)
            j = l - N_A
            mix = stick_breaking_mixer(x, b_w_q[j], b_w_o[j], k_shared, v_shared)
        x = layer_norm(ALPHA * x + mix, mix_ln_g[l], mix_ln_b[l])
        x = layer_norm(ALPHA * x + squared_relu_mlp(x, ffn_w1[l], ffn_w2[l]), ffn_ln_g[l], ffn_ln_b[l])
    return x
```

```python
import numpy as np
from contextlib import ExitStack
import concourse.bass as bass
import concourse.mybir as mybir
from concourse.bass_utils import run_bass_kernel_spmd

F32 = mybir.dt.float32
BF16 = mybir.dt.bfloat16
AF = mybir.ActivationFunctionType
ALU = mybir.AluOpType

D = 1024
DFF = 4096
TT = 512
ALPHA = float(8 ** 0.25)
LN_EPS = 1e-5
NEG_BIG = -30000.0
N_A = 2
LOOKAHEAD = 3
NSPLIT = 2
GATE_ENG = "pool"
ADDB_ENG = "dve"
NWBUF = 4


class Tok:
    __slots__ = ("sem", "value", "eng")

    def __init__(self, sem, eng, value=None):
        self.sem = sem
        self.eng = eng
        self.value = value


class Prog:
    ENG = ("pe", "act", "dve", "pool", "sp")

    def __init__(self, nc, es):
        self.nc = nc
        self.es = es
        self.lists = {e: [] for e in self.ENG}
        self.sems = {}
        for e in ("pe", "act", "dve", "pool"):
            self.sems["c_" + e] = es.enter_context(nc.semaphore("c_" + e))
        self.cnt = {e: 0 for e in self.ENG}
        self.cur = {e: Tok("c_" + e, e) for e in ("pe", "act", "dve", "pool")}
        self.waited = {}
        self.lastw = {}
        self.readers = {}
        self.dcnt = {}
        self.ninstr = 0

    def dsem(self, name):
        if name not in self.sems:
            self.sems[name] = self.es.enter_context(self.nc.semaphore(name))
            self.dcnt[name] = 0
        return name

    def _deps(self, eng, reads, writes):
        toks = []
        for k in reads:
            t = self.lastw.get(k)
            if t is not None:
                toks.append(t)
        for k in writes:
            t = self.lastw.get(k)
            if t is not None:
                toks.append(t)
            toks.extend(self.readers.get(k, ()))
        need = {}
        for t in toks:
            if t.eng == "pe" and eng == "pe":
                continue
            if t.value is None:
                raise RuntimeError(f"dependency on unsignaled op of {t.eng} from {eng}")
            if self.waited.get((eng, t.sem), 0) >= t.value:
                continue
            if need.get(t.sem, 0) < t.value:
                need[t.sem] = t.value
        for s, v in need.items():
            self.waited[(eng, s)] = v
        return list(need.items())

    def _record(self, tok, reads, writes):
        for k in reads:
            self.readers.setdefault(k, []).append(tok)
        for k in writes:
            self.lastw[k] = tok
            self.readers[k] = []

    def op(self, eng, fn, reads=(), writes=(), signal=True):
        waits = self._deps(eng, reads, writes)
        tok = self.cur[eng]
        if signal:
            self.cnt[eng] += 1
            tok.value = self.cnt[eng]
            self.cur[eng] = Tok("c_" + eng, eng)
        self.lists[eng].append((waits, fn, ("c_" + eng, 1) if signal else None))
        self._record(tok, reads, writes)
        self.ninstr += 1
        return tok

    def dma(self, eng, fn, sem, reads=(), writes=()):
        self.dsem(sem)
        waits = self._deps(eng, reads, writes)
        self.dcnt[sem] += 16
        tok = Tok(sem, "dma", self.dcnt[sem])
        self.lists[eng].append((waits, fn, (sem, 16)))
        self._record(tok, reads, writes)
        self.ninstr += 1
        return tok

    def wait_tok(self, eng, tok):
        if self.waited.get((eng, tok.sem), 0) >= tok.value:
            return
        self.waited[(eng, tok.sem)] = tok.value
        self.lists[eng].append(([(tok.sem, tok.value)], None, None))

    def emit(self, block):
        nc = self.nc
        engmap = {"pe": block.tensor, "act": block.scalar, "dve": block.vector, "pool": block.gpsimd, "sp": block.sync}
        for ename, deco in engmap.items():
            lst = self.lists[ename]
            sems = self.sems

            def body(e, lst=lst):
                for waits, fn, inc in lst:
                    for s, v in waits:
                        e.wait_ge(sems[s], v)
                    if fn is None:
                        continue
                    ins = fn(e)
                    if inc is not None:
                        ins.then_inc(sems[inc[0]], inc[1])
            deco(body)


def build(NT, layers=(0, 1, 2, 3), kv_in=False, kv_out=False):
    S = NT * TT
    layers = tuple(layers)
    need_kv_proj = (1 in layers)
    has_b = any(l >= N_A for l in layers)
    has_a = any(l < N_A for l in layers)
    nc = bass.Bass("TRN2", target_bir_lowering=False)

    def din(name, shape, dt=F32):
        return nc.dram_tensor(name, list(shape), dt, kind="ExternalInput").ap()

    xin = din("xin", [S, D])
    xout = nc.dram_tensor("xout", [S, D], F32, kind="ExternalOutput").ap()
    Wd = {}
    Wd["mix_ln_g"] = din("mix_ln_g", [4, D]); Wd["mix_ln_b"] = din("mix_ln_b", [4, D])
    Wd["ffn_ln_g"] = din("ffn_ln_g", [4, D]); Wd["ffn_ln_b"] = din("ffn_ln_b", [4, D])
    Wd["ffn_w1"] = din("ffn_w1", [4, D, DFF]); Wd["ffn_w2"] = din("ffn_w2", [4, DFF, D])
    if has_a:
        Wd["a_w_in"] = din("a_w_in", [2, D, 2 * D]); Wd["a_ln_g"] = din("a_ln_g", [2, D]); Wd["a_ln_b"] = din("a_ln_b", [2, D])
        Wd["a_w_s"] = din("a_w_s", [2, 8, 128, 128]); Wd["a_b_s"] = din("a_b_s", [2, 8 * 128]); Wd["a_w_out"] = din("a_w_out", [2, D, D])
    if need_kv_proj:
        Wd["sb_w_k"] = din("sb_w_k", [D, D]); Wd["sb_w_v"] = din("sb_w_v", [D, D])
    if has_b:
        Wd["b_w_q"] = din("b_w_q", [2, D, D]); Wd["b_w_o"] = din("b_w_o", [2, D, D])
    if need_kv_proj or has_b:
        kvkind = "ExternalInput" if kv_in else ("ExternalOutput" if kv_out else "Internal")
        kTd = nc.dram_tensor("kTd", [8, 128, S], BF16, kind=kvkind).ap()
        Vd = nc.dram_tensor("Vd", [S, D], BF16, kind=kvkind).ap()

    es = ExitStack()
    with es:
        def sb(name, shape, dt):
            return es.enter_context(nc.sbuf_tensor(name, list(shape), dt))

        ident = sb("ident", [128, 128], BF16)
        negtri = sb("negtri", [128, 128], BF16)
        negones = sb("negones", [128, 128], BF16)
        maskd = sb("maskd", [128, 4, 512], BF16)
        wsT = sb("wsT", [128, 2, 8, 128], BF16)
        bsb = sb("bsb", [128, 2, 8, 128], F32)
        x = sb("x", [128, 4, D], F32)
        xb = sb("xb", [128, 4, D], BF16)
        xT = sb("xT", [128, 8, TT], BF16)
        gbc = sb("gbc", [128, 2, 2, D], F32)
        uT = sb("uT", [128, 8, TT], BF16)
        vf = sb("vf", [128, 4, D], F32)
        vb = sb("vb", [128, 4, D], BF16)
        R1 = sb("R1", [128, 32 * TT], BF16)
        ebuf = sb("ebuf", [128, 2, 2, TT], F32)
        lbuf = sb("lbuf", [128, 2, 2, TT], BF16)
        lsum = sb("lsum", [128, 3, 2, TT], BF16)
        abuf = sb("abuf", [128, 2, 2, TT], BF16)
        qT = sb("qT", [128, 8, TT], BF16)
        oT = sb("oT", [128, 8, TT], BF16)
        wbuf = sb("wbuf", [128, NWBUF, 8, TT], BF16)
        st = sb("st", [128, 4, 12], F32)
        mv = sb("mv", [128, 4, 2], F32)
        lnv = sb("lnv", [128, 4], F32)
        rstd = sb("rstd", [128, 4], F32)
        nmr = sb("nmr", [128, 4], F32)
        PS = es.enter_context(nc.psum_tensor("PS", [128, 8, 512], F32))
        pT = PS[:, 7, :].bitcast(BF16).rearrange("p (a t) -> p a t", a=2)

        P = Prog(nc, es)
        cf = ebuf[:, :, :, :].rearrange("p a b t -> p (a b) t")
        wsf = vf[:, 0, :].rearrange("p (g t) -> p g t", t=128)
        wsb = vb[:, 0, :].rearrange("p (g t) -> p g t", t=128)

        def hT(ffc):
            return R1[:, ffc * TT:(ffc + 1) * TT]

        def kTb(slot):
            return R1[:, slot * 4096:(slot + 1) * 4096]

        def Vb(slot, kb):
            o = 8192 + slot * 4096 + kb * 128
            return R1[:, o:o + 128]

        def r1keys_h(ffc):
            return ["R1_%d" % ffc]

        def r1keys_kT(slot):
            return ["R1_%d" % i for i in range(slot * 8, slot * 8 + 8)]

        def r1keys_V(slot):
            return ["R1_%d" % i for i in range(16 + slot * 8, 16 + slot * 8 + 8)]

        P.op("pool", lambda e: e.memset(cf[:, 0, 0:128], 0.0), writes=["ebuf0"])
        P.op("pool", lambda e: e.affine_select(out=cf[:, 0, 0:128], in_=cf[:, 0, 0:128], pattern=[[-1, 128]],
                                                compare_op=ALU.not_equal, fill=1.0, base=0, channel_multiplier=1),
             reads=["ebuf0"], writes=["ebuf0"])
        P.op("dve", lambda e: e.tensor_copy(ident[:], cf[:, 0, 0:128]), reads=["ebuf0"], writes=["ident"])
        if has_b:
            P.op("pool", lambda e: e.memset(cf[:, 1, 0:128], -1.0), reads=["ebuf0"], writes=["ebuf0"])
            P.op("pool", lambda e: e.affine_select(out=cf[:, 1, 0:128], in_=cf[:, 1, 0:128], pattern=[[-1, 128]],
                                                    compare_op=ALU.is_ge, fill=0.0, base=0, channel_multiplier=1),
                 reads=["ebuf0"], writes=["ebuf0"])
            P.op("dve", lambda e: e.tensor_copy(negtri[:], cf[:, 1, 0:128]), reads=["ebuf0"], writes=["negtri"])
            P.op("dve", lambda e: e.memset(negones[:], -1.0), writes=["negones"])
            P.op("pool", lambda e: e.memset(cf[:, :, :], NEG_BIG), reads=["ebuf0", "ebuf1"], writes=["ebuf0", "ebuf1"])
            P.op("pool", lambda e: e.affine_select(out=cf[:, :, :], in_=cf[:, :, :], pattern=[[128, 4], [-1, 512]],
                                                    compare_op=ALU.is_ge, fill=0.0, base=0, channel_multiplier=1),
                 reads=["ebuf0", "ebuf1"], writes=["ebuf0", "ebuf1"])
            P.op("dve", lambda e: e.tensor_copy(maskd[:, :, :], cf[:, :, :]), reads=["ebuf0", "ebuf1"], writes=["maskd"])
        if has_a:
            for l in range(2):
                P.dma("sp", lambda e, l=l: e.dma_start(out=wsf[:, :, :], in_=Wd["a_w_s"][l].rearrange("g t s -> t g s")),
                      "cst_ws%d" % l, writes=["bB0"])
                P.op("dve", lambda e: e.tensor_copy(wsb[:, :, :], wsf[:, :, :]), reads=["bB0"], writes=["vb0"])
                for g in range(8):
                    P.op("pe", lambda e, g=g: e.transpose(pT[:, g // 4, (g % 4) * 128:(g % 4 + 1) * 128], wsb[:, g, :], ident[:]),
                         reads=["vb0", "ident"], writes=["ps7"], signal=(g == 7))
                P.op("dve", lambda e, l=l: e.tensor_copy(wsT[:, l, :, :], pT[:, :, :].rearrange("p a (q t) -> p (a q) t", t=128)), reads=["ps7"], writes=["wsT"])
                P.op("dve", lambda e, l=l: e.memset(wsT[64:128, l, :, 0:64], 0.0), reads=["wsT"], writes=["wsT"])
                P.dma("sp", lambda e, l=l: e.dma_start(out=bsb[:, l, :, :].rearrange("p g t -> p (g t)"), in_=Wd["a_b_s"][l:l + 1, :].broadcast_to([128, 1024])),
                      "cst_bs%d" % l, writes=["bsb"])

        def piece_src(p):
            kind = p[0]
            if kind == "win_u":
                return Wd["a_w_in"][p[1]], 0, p[2] * 512
            if kind == "win_v":
                return Wd["a_w_in"][p[1]], 0, 1024 + p[2] * 512
            if kind == "wout":
                return Wd["a_w_out"][p[1]], 0, p[2] * 512
            if kind == "w1":
                return Wd["ffn_w1"][p[1]], 0, p[2] * 512
            if kind == "w2":
                return Wd["ffn_w2"][p[1]], p[3] * 8, p[2] * 512
            if kind == "wk":
                return Wd["sb_w_k"], 0, p[1] * 512
            if kind == "wv":
                return Wd["sb_w_v"], 0, p[1] * 512
            if kind == "wq":
                return Wd["b_w_q"][p[1]], 0, p[2] * 512
            if kind == "wo":
                return Wd["b_w_o"][p[1]], 0, p[2] * 512
            raise KeyError(kind)

        def pieces_ffn(l):
            return [("w1", l, g) for g in range(8)] + [("w2", l, h, g) for h in range(2) for g in range(4)]

        def pieces_tile():
            out = []
            for l in layers:
                if l < N_A:
                    out += [("win_v", l, 0), ("win_v", l, 1), ("win_u", l, 0), ("win_u", l, 1), ("wout", l, 0), ("wout", l, 1)]
                    out += pieces_ffn(l)
                    if l == 1:
                        out += [("wk", 0), ("wk", 1), ("wv", 0), ("wv", 1)]
                else:
                    j = l - N_A
                    out += [("wq", j, 0), ("wq", j, 1), ("wo", j, 0), ("wo", j, 1)]
                    out += pieces_ffn(l)
            return out

        sched = []
        for J in range(NT):
            sched += pieces_tile()
        wstate = {"issued": 0, "next": 0}

        def issue_piece(i):
            p = sched[i]
            slot = i % NWBUF
            W2, r0, c0 = piece_src(p)
            src = W2[r0 * 128:(r0 + 8) * 128, c0:c0 + 512].rearrange("(k p) c -> p k c", p=128)
            P.dma("pool", lambda e, src=src, slot=slot: e.dma_start(out=wbuf[:, slot, :, :], in_=src),
                  "w%d" % slot, writes=["wbuf%d" % slot])

        def get_piece(expect, keep_prev=False):
            i = wstate["next"]
            assert sched[i] == expect, (sched[i], expect)
            released = i - (1 if keep_prev else 0)
            while (wstate["issued"] < len(sched) and wstate["issued"] < i + 1 + LOOKAHEAD
                   and wstate["issued"] - NWBUF < released):
                issue_piece(wstate["issued"])
                wstate["issued"] += 1
            assert wstate["issued"] > i
            wstate["next"] += 1
            slot = i % NWBUF
            return slot, "wbuf%d" % slot

        rot = {"i": 0}

        def rotbank():
            b = 4 + rot["i"] % 3
            rot["i"] += 1
            return b

        gbstate = {"i": 0}

        def load_gb(gname, bname, l):
            slot = gbstate["i"] % 2
            gbstate["i"] += 1
            key = "gb%d" % slot
            P.dma("sp", lambda e: e.dma_start(out=gbc[:, slot, 0, :], in_=Wd[gname][l:l + 1, :].broadcast_to([128, D])),
                  "gbsg%d" % slot, writes=[key])
            P.dma("sp", lambda e: e.dma_start(out=gbc[:, slot, 1, :], in_=Wd[bname][l:l + 1, :].broadcast_to([128, D])),
                  "gbsb%d" % slot, writes=[key + "b"])
            return slot

        XT_KEYS = ["xTs%d" % s_ for s_ in range(4)]
        trc = {"i": 0}

        def tr_sub(s):
            b = rotbank()
            tb = PS[:, b, :].bitcast(BF16).rearrange("p (k t) -> p k t", t=128)
            for k in range(8):
                P.op("pe", lambda e, s=s, k=k, tb=tb: e.transpose(tb[:, k, :], xb[:, s, k * 128:(k + 1) * 128], ident[:]),
                     reads=["xb%d" % s, "ident"], writes=["ps%d" % b], signal=(k == 7))
            P.op("act", lambda e, s=s, tb=tb: e.copy(xT[:, :, s * 128:(s + 1) * 128], tb[:, :, :]), reads=["ps%d" % b], writes=["xTs%d" % s])

        class LNPipe:
            def __init__(self, stages):
                self.stages = stages
                self.i = 0

            def step(self):
                i = self.i
                for k_, stage in enumerate(self.stages):
                    s_ = i - k_
                    if 0 <= s_ < 4:
                        stage(s_)
                self.i += 1

            def finish(self):
                while self.i < 4 + len(self.stages) - 1:
                    self.step()

        def layer_norm(buf, bkey, gslot, out_bf, okey, with_f32=True, do_xT=True, need_bf=True):
            gk, bk = "gb%d" % gslot, "gb%db" % gslot

            def stats(s):
                P.op("dve", lambda e: e.bn_stats(st[:, s, 0:6], buf[:, s, 0:512]), reads=[bkey % s], writes=["st%d" % s])
                P.op("dve", lambda e: e.bn_stats(st[:, s, 6:12], buf[:, s, 512:1024]), reads=[bkey % s], writes=["st%d" % s])
                P.op("dve", lambda e: e.bn_aggr(mv[:, s, :], st[:, s, :]), reads=["st%d" % s], writes=["mv%d" % s])

            def rs(s):
                P.op("act", lambda e: e.activation(out=lnv[:, s:s + 1], in_=mv[:, s, 1:2], func=AF.Ln, bias=LN_EPS), reads=["mv%d" % s], writes=["lnv%d" % s])
                P.op("act", lambda e: e.activation(out=rstd[:, s:s + 1], in_=lnv[:, s:s + 1], func=AF.Exp, scale=-0.5), reads=["lnv%d" % s], writes=["rstd%d" % s])

            def opA(s):
                P.op("dve", lambda e: e.scalar_tensor_tensor(out=buf[:, s, :], in0=buf[:, s, :], scalar=mv[:, s, 0:1], in1=gbc[:, gslot, 0, :],
                                                             op0=ALU.subtract, op1=ALU.mult),
                     reads=[bkey % s, "mv%d" % s, gk], writes=[bkey % s])

            def opB(s):
                if with_f32:
                    P.op("dve", lambda e: e.scalar_tensor_tensor(out=buf[:, s, :], in0=buf[:, s, :], scalar=rstd[:, s:s + 1], in1=gbc[:, gslot, 1, :],
                                                                 op0=ALU.mult, op1=ALU.add),
                         reads=[bkey % s, "rstd%d" % s, bk], writes=[bkey % s])
                else:
                    P.op("dve", lambda e: e.scalar_tensor_tensor(out=out_bf[:, s, :], in0=buf[:, s, :], scalar=rstd[:, s:s + 1], in1=gbc[:, gslot, 1, :],
                                                                 op0=ALU.mult, op1=ALU.add),
                         reads=[bkey % s, "rstd%d" % s, bk], writes=[okey % s])

            def cpy(s):
                P.op("act", lambda e: e.copy(out_bf[:, s, :], buf[:, s, :]), reads=[bkey % s], writes=[okey % s])

            stages = [stats, rs, opA, opB]
            if with_f32 and need_bf:
                stages.append(cpy)
                if do_xT:
                    stages.append(tr_sub)
            return LNPipe(stages)

        def fm_piece(pi, slot, wkey, srcT, evac_j, split):
            if split:
                banks = [0, 1, 2, 3] if pi % 2 == 0 else [4, 5, 6, 7]
                for hf in range(2):
                    rk = ["xTs%d" % (2 * hf), "xTs%d" % (2 * hf + 1)]
                    for j in range(4):
                        b = banks[j]
                        for k in range(8):
                            P.op("pe", lambda e, j=j, k=k, b=b, hf=hf: e.matmul(PS[:, b, hf * 256:(hf + 1) * 256], wbuf[:, slot, k, j * 128:(j + 1) * 128],
                                                                            srcT[:, k, hf * 256:(hf + 1) * 256], start=(k == 0), stop=(k == 7)),
                                 reads=[wkey] + rk, writes=["ps%d" % b], signal=(k == 7))
                        if hf == 1:
                            evac_j(j, b)
            else:
                for j in range(4):
                    b = rotbank()
                    for k in range(8):
                        P.op("pe", lambda e, j=j, k=k, b=b: e.matmul(PS[:, b, :], wbuf[:, slot, k, j * 128:(j + 1) * 128], srcT[:, k, :], start=(k == 0), stop=(k == 7)),
                             reads=[wkey] + XT_KEYS, writes=["ps%d" % b], signal=(k == 7))
                    evac_j(j, b)

        def gemm_fm(pieces, srcT, src_keys, evac, nsplit=0):
            for pi, pc in enumerate(pieces):
                slot, wkey = get_piece(pc)
                fm_piece(pi, slot, wkey, srcT, (lambda j, b, pi=pi: evac(pi * 4 + j, b)), split=(pi < nsplit))

        def gemm_tm(pieces, srcT, src_keys, evac, after_s=None):
            slots = [get_piece(pieces[0]), get_piece(pieces[1], keep_prev=True)]
            for s in range(4):
                for half in range(2):
                    slot, wkey = slots[half]
                    b = (2 * s + half) % 4
                    for k in range(8):
                        P.op("pe", lambda e, slot=slot, s=s, k=k, b=b: e.matmul(PS[:, b, :], srcT[:, k, s * 128:(s + 1) * 128], wbuf[:, slot, k, :], start=(k == 0), stop=(k == 7)),
                             reads=[wkey] + src_keys(s, k), writes=["ps%d" % b], signal=(k == 7))
                    evac(s, half, b)
                if after_s is not None:
                    after_s(s)

        cur = {"x": x, "xk": "bA%d", "v": vf, "vk": "bB%d"}

        def evac_resid(s, half, b):
            xx, xk = cur["x"], cur["xk"]
            P.op("dve", lambda e: e.scalar_tensor_tensor(out=xx[:, s, half * 512:(half + 1) * 512], in0=xx[:, s, half * 512:(half + 1) * 512], scalar=ALPHA,
                                                         in1=PS[:, b, :], op0=ALU.mult, op1=ALU.add),
                 reads=[xk % s, "ps%d" % b], writes=[xk % s])

        def ffn(l, need_xT=True, mid_hook=None):
            gslot = load_gb("ffn_ln_g", "ffn_ln_b", l)
            def evac_h(g, j, b):
                ffc = g * 4 + j
                es_ = ffc % 2
                P.op("act", lambda e: e.activation(out=ebuf[:, es_, 0, :], in_=PS[:, b, :], func=AF.Relu),
                     reads=["ps%d" % b], writes=["ebuf%d" % es_])
                P.op("dve", lambda e: e.tensor_tensor(out=hT(ffc), in0=ebuf[:, es_, 0, :], in1=ebuf[:, es_, 0, :], op=ALU.mult),
                     reads=["ebuf%d" % es_], writes=r1keys_h(ffc))
            for g in range(8):
                slot, wkey = get_piece(("w1", l, g))
                fm_piece(g, slot, wkey, xT, (lambda j, b, g=g: evac_h(g, j, b)), split=(g < NSPLIT))
            if mid_hook is not None:
                mid_hook()
            lnp = layer_norm(cur["x"], cur["xk"], gslot, xb, "xb%d", do_xT=need_xT, need_bf=need_xT)
            for half in range(2):
                for g8 in range(4):
                    slot, wkey = get_piece(("w2", l, half, g8))
                    for s in range(4):
                        for c in range(8):
                            ffc = g8 * 8 + c
                            P.op("pe", lambda e, slot=slot, s=s, c=c, ffc=ffc, g8=g8: e.matmul(PS[:, s, :], hT(ffc)[:, s * 128:(s + 1) * 128], wbuf[:, slot, c, :],
                                                                                          start=(g8 == 0 and c == 0), stop=(g8 == 3 and c == 7)),
                                 reads=[wkey] + r1keys_h(ffc), writes=["ps%d" % s], signal=(c == 7))
                        if g8 == 3:
                            evac_resid(s, half, s)
                            if half == 1:
                                lnp.step()
            lnp.finish()

        def gmlp(l):
            gslot_v = load_gb("a_ln_g", "a_ln_b", l)
            gslot_m = load_gb("mix_ln_g", "mix_ln_b", l)

            vv, vk = cur["v"], cur["vk"]

            def evac_v(s, half, b):
                P.op("act", lambda e: e.activation(out=vv[:, s, half * 512:(half + 1) * 512], in_=PS[:, b, :], func=AF.Gelu_apprx_tanh),
                     reads=["ps%d" % b], writes=[vk % s])
            lnv_ = layer_norm(vv, vk, gslot_v, vb, "vb%d", with_f32=False)
            gemm_tm([("win_v", l, 0), ("win_v", l, 1)], xT, lambda s_, k_: ["xTs%d" % s_], evac_v, after_s=lambda s_: lnv_.step())

            def evac_u(oc, b):
                P.op("act", lambda e: e.activation(out=uT[:, oc, :], in_=PS[:, b, :], func=AF.Gelu_apprx_tanh),
                     reads=["ps%d" % b], writes=["uT%d" % oc])
            lnv_.finish()
            gemm_fm([("win_u", l, 0), ("win_u", l, 1)], xT, XT_KEYS, evac_u)
            for g in range(8):
                b = g
                for s in range(4):
                    P.op("pe", lambda e, g=g, s=s, b=b: e.matmul(PS[:, b, s * 128:(s + 1) * 128], vb[:, s, g * 128:(g + 1) * 128], wsT[:, l, g, :], start=True, stop=True),
                         reads=["vb%d" % s, "wsT"], writes=["ps%d" % b], signal=(s == 3))
                es_ = g % 2
                P.op("dve", lambda e, g=g, b=b, es_=es_: e.tensor_tensor(out=ebuf[:, es_, 0, :].rearrange("p (s t) -> p s t", t=128),
                                                                       in0=PS[:, b, :].rearrange("p (s t) -> p s t", t=128),
                                                                       in1=bsb[:, l, g:g + 1, :].broadcast_to([128, 4, 128]), op=ALU.add),
                     reads=["ps%d" % b, "bsb"], writes=["ebuf%d" % es_])
                P.op(GATE_ENG, lambda e, g=g, es_=es_: e.tensor_tensor(out=uT[:, g, :], in0=ebuf[:, es_, 0, :], in1=uT[:, g, :], op=ALU.mult),
                     reads=["ebuf%d" % es_, "uT%d" % g], writes=["uT%d" % g])
            lnm_ = layer_norm(cur["x"], cur["xk"], gslot_m, xb, "xb%d")
            gemm_tm([("wout", l, 0), ("wout", l, 1)], uT, lambda s_, k_: ["uT%d" % k_], evac_resid, after_s=lambda s_: lnm_.step())
            lnm_.finish()

        def kv_proj(J):
            def evac_k(oc, b):
                P.op("act", lambda e: e.copy(uT[:, oc, :], PS[:, b, :]), reads=["ps%d" % b], writes=["uT%d" % oc])
            gemm_fm([("wk", 0), ("wk", 1)], xT, XT_KEYS, evac_k, nsplit=NSPLIT)

            def evac_vv(s, half, b):
                P.op("dve", lambda e: e.tensor_copy(vb[:, s, half * 512:(half + 1) * 512], PS[:, b, :]), reads=["ps%d" % b], writes=["vb%d" % s])
            gemm_tm([("wv", 0), ("wv", 1)], xT, lambda s_, k_: ["xTs%d" % s_], evac_vv)
            P.dma("sp", lambda e: e.dma_start(out=kTd.rearrange("c p s -> p c s")[:, :, J * TT:(J + 1) * TT], in_=uT[:, :, :]),
                  "kst", reads=["uT%d" % k for k in range(8)], writes=["kd%d" % J])
            P.dma("sp", lambda e: e.dma_start(out=Vd[J * TT:(J + 1) * TT, :].rearrange("(s p) d -> p s d", p=128), in_=vb[:, :, :]),
                  "vst", reads=["vb%d" % s for s in range(4)], writes=["vd%d" % J])

        def load_kv_chunk(J, c):
            slot = c % 2
            n = (J + 1) * TT
            nkb = 4 * (J + 1)
            P.dma("sp", lambda e: e.dma_start(out=kTb(slot)[:, 0:n], in_=kTd[c][:, 0:n]),
                  "kl%d" % slot, reads=["kd%d" % t for t in range(J + 1)], writes=r1keys_kT(slot))
            vdst = R1[:, 8192 + slot * 4096: 8192 + slot * 4096 + nkb * 128].rearrange("p (b d) -> p b d", d=128)
            P.dma("sp", lambda e: e.dma_start(out=vdst, in_=Vd[0:n, c * 128:(c + 1) * 128].rearrange("(b p) d -> p b d", p=128)),
                  "vl%d" % slot, reads=["vd%d" % t for t in range(J + 1)], writes=r1keys_V(slot))

        def attention(J):
            nkb = 4 * (J + 1)
            U = nkb
            G = 8 * U

            def info(g):
                c, u = divmod(g, U)
                kb = nkb - 1 - u
                i = kb - 4 * J
                return c, u, kb, i, (i * 128 if i > 0 else 0)

            def Zop(g):
                c, u, kb, i, c0 = info(g)
                slot = c % 2
                zb = 2 * (g % 3)
                for h in range(2):
                    P.op("pe", lambda e, h=h: e.matmul(PS[:, zb + h, c0:512], kTb(slot)[h * 64:(h + 1) * 64, kb * 128:(kb + 1) * 128], qT[h * 64:(h + 1) * 64, c, c0:512],
                                                       start=True, stop=(i < 0)),
                         reads=r1keys_kT(slot) + ["qT%d" % c], writes=["ps%d" % (zb + h)], signal=(h == 1 and i < 0))
                if i >= 0:
                    for h in range(2):
                        P.op("pe", lambda e, h=h: e.matmul(PS[:, zb + h, c0:512], ident[:], maskd[:, i, c0:512], start=False, stop=True),
                             reads=["ident", "maskd"], writes=["ps%d" % (zb + h)], signal=(h == 1))

            def TRIop(g):
                c, u, kb, i, c0 = info(g)
                zb = 2 * (g % 3)
                for h in range(2):
                    P.op("pe", lambda e, h=h: e.matmul(PS[:, zb + h, c0:512], negtri[:], lbuf[:, g % 2, h, c0:512], start=False, stop=True, skip_group_check=True),
                         reads=["negtri", "lbuf%d" % (g % 2)], writes=["ps%d" % (zb + h)], signal=(u == 0 and h == 1))
                if u > 0:
                    pc0 = info(g - 1)[4]
                    for h in range(2):
                        P.op("pe", lambda e, h=h: e.matmul(PS[:, zb + h, pc0:512], negones[:], lsum[:, (g - 1) % 3, h, pc0:512], start=False, stop=True, skip_group_check=True),
                             reads=["negones", "lsum%d" % ((g - 1) % 3)], writes=["ps%d" % (zb + h)], signal=(h == 1))

            def AVop(g):
                c, u, kb, i, c0 = info(g)
                slot = c % 2
                ob = 6 + (c % 2)
                for h in range(2):
                    P.op("pe", lambda e, h=h: e.matmul(PS[h * 64:(h + 1) * 64, ob, c0:512], Vb(slot, kb)[:, h * 64:(h + 1) * 64], abuf[:, g % 2, h, c0:512],
                                                       start=(u == 0), stop=(u == U - 1), skip_group_check=True),
                         reads=r1keys_V(slot) + ["abuf%d" % (g % 2)], writes=["ps%d" % ob], signal=(h == 1))
                if u == U - 1:
                    P.op("dve", lambda e: e.tensor_copy(oT[:, c, :], PS[:, ob, :]), reads=["ps%d" % ob], writes=["oT%d" % c])
                    if c + 2 < 8:
                        load_kv_chunk(J, c + 2)

            def E1L(g):
                c, u, kb, i, c0 = info(g)
                zb = 2 * (g % 3)
                P.op("act", lambda e: e.activation(out=ebuf[:, g % 2, :, c0:512], in_=PS[:, zb:zb + 2, c0:512], func=AF.Exp),
                     reads=["ps%d" % zb, "ps%d" % (zb + 1)], writes=["ebuf%d" % (g % 2)])
                P.op("act", lambda e: e.activation(out=lbuf[:, g % 2, :, c0:512], in_=ebuf[:, g % 2, :, c0:512], func=AF.Ln, bias=1.0),
                     reads=["ebuf%d" % (g % 2)], writes=["lbuf%d" % (g % 2)])

            def E2(g):
                c, u, kb, i, c0 = info(g)
                zb = 2 * (g % 3)
                P.op("act", lambda e: e.activation(out=abuf[:, g % 2, :, c0:512], in_=PS[:, zb:zb + 2, c0:512], func=AF.Exp),
                     reads=["ps%d" % zb, "ps%d" % (zb + 1)], writes=["abuf%d" % (g % 2)])

            def LS(g):
                c, u, kb, i, c0 = info(g)
                if u + 1 >= U:
                    return
                cur, prv = g % 3, (g - 1) % 3
                if u == 0:
                    P.op("dve", lambda e: e.tensor_copy(lsum[:, cur, :, c0:512], lbuf[:, g % 2, :, c0:512]), reads=["lbuf%d" % (g % 2)], writes=["lsum%d" % cur])
                else:
                    pc0 = info(g - 1)[4]
                    P.op("dve", lambda e: e.tensor_tensor(out=lsum[:, cur, :, pc0:512], in0=lsum[:, prv, :, pc0:512], in1=lbuf[:, g % 2, :, pc0:512], op=ALU.add),
                         reads=["lsum%d" % prv, "lbuf%d" % (g % 2)], writes=["lsum%d" % cur])
                    if pc0 > c0:
                        P.op("dve", lambda e: e.tensor_copy(lsum[:, cur, :, c0:pc0], lbuf[:, g % 2, :, c0:pc0]), reads=["lbuf%d" % (g % 2)], writes=["lsum%d" % cur])

            load_kv_chunk(J, 0)
            load_kv_chunk(J, 1)
            Zop(0)
            for step in range(G + 2):
                if 0 <= step - 1 < G:
                    TRIop(step - 1)
                if step + 1 < G:
                    Zop(step + 1)
                if 0 <= step - 2 < G:
                    AVop(step - 2)
                if step < G:
                    E1L(step)
                if 0 <= step - 1 < G:
                    E2(step - 1)
                if step < G:
                    LS(step)

        def sb_layer(l, J):
            j = l - N_A
            gslot_m = load_gb("mix_ln_g", "mix_ln_b", l)

            def evac_q(oc, b):
                P.op("act", lambda e: e.activation(out=qT[:, oc, :], in_=PS[:, b, :], func=AF.Identity, scale=0.125),
                     reads=["ps%d" % b], writes=["qT%d" % oc])
            gemm_fm([("wq", j, 0), ("wq", j, 1)], xT, XT_KEYS, evac_q, nsplit=NSPLIT)
            attention(J)
            lnm_ = layer_norm(cur["x"], cur["xk"], gslot_m, xb, "xb%d")
            gemm_tm([("wo", j, 0), ("wo", j, 1)], oT, lambda s_, k_: ["oT%d" % k_], evac_resid, after_s=lambda s_: lnm_.step())
            lnm_.finish()

        bufs = [(x, "bA%d"), (vf, "bB%d")]

        def load_x(J):
            dst, kf = bufs[J % 2]
            P.dma("sp", lambda e: e.dma_start(out=dst[:, :, :], in_=xin[J * TT:(J + 1) * TT, :].rearrange("(s p) d -> p s d", p=128)),
                  "xld", writes=[kf % s for s in range(4)])

        def prologue(J):
            src, kf = bufs[J % 2]
            for s in range(4):
                P.op("act", lambda e, s=s: e.copy(xb[:, s, :], src[:, s, :]), reads=[kf % s], writes=["xb%d" % s])
                tr_sub(s)

        a_layers = [l for l in layers if l < N_A]
        store_tok = None
        load_x(0)
        prologue(0)
        for J in range(NT):
            cur["x"], cur["xk"] = bufs[J % 2]
            cur["v"], cur["vk"] = bufs[(J + 1) % 2]
            nxt = (J + 1 < NT)
            if nxt and not a_layers:
                load_x(J + 1)
            hooked = False
            for l in layers:
                last = (l == layers[-1])
                hook = None
                if last and nxt and l != 1:
                    hook = (lambda J=J: prologue(J + 1))
                    hooked = True
                if l < N_A:
                    gmlp(l)
                    if nxt and l == a_layers[-1]:
                        load_x(J + 1)
                    ffn(l, need_xT=(l == 1 or not last), mid_hook=hook)
                    if l == 1:
                        kv_proj(J)
                else:
                    sb_layer(l, J)
                    ffn(l, need_xT=not last, mid_hook=hook)
            xs, xkf = cur["x"], cur["xk"]
            store_tok = P.dma("sp", lambda e, J=J, xs=xs: e.dma_start(out=xout[J * TT:(J + 1) * TT, :].rearrange("(s p) d -> p s d", p=128), in_=xs[:, :, :]),
                              "xst", reads=[xkf % s for s in range(4)])
            if nxt and not hooked:
                prologue(J + 1)
        P.wait_tok("sp", store_tok)
        if need_kv_proj and kv_out:
            P.wait_tok("sp", Tok("kst", "dma", P.dcnt["kst"]))
            P.wait_tok("sp", Tok("vst", "dma", P.dcnt["vst"]))
        block = es.enter_context(nc.Block())
        P.emit(block)
    return nc


_WNAMES_ALL = ["a_w_in", "a_ln_g", "a_ln_b", "a_w_s", "a_b_s", "a_w_out", "sb_w_k", "sb_w_v", "b_w_q", "b_w_o",
               "mix_ln_g", "mix_ln_b", "ffn_ln_g", "ffn_ln_b", "ffn_w1", "ffn_w2"]


def _wmap(inputs, layers):
    layers = tuple(layers)
    has_a = any(l < N_A for l in layers)
    has_b = any(l >= N_A for l in layers)
    names = ["mix_ln_g", "mix_ln_b", "ffn_ln_g", "ffn_ln_b", "ffn_w1", "ffn_w2"]
    if has_a:
        names += ["a_w_in", "a_ln_g", "a_ln_b", "a_w_s", "a_b_s", "a_w_out"]
    if 1 in layers:
        names += ["sb_w_k", "sb_w_v"]
    if has_b:
        names += ["b_w_q", "b_w_o"]
    m = {}
    for n in names:
        a = np.ascontiguousarray(np.asarray(inputs[n], dtype=np.float32))
        if n == "a_b_s":
            a = a.reshape(2, 8 * 128)
        m[n] = a
    return m


FUSED = True


def kernel(**inputs):
    x = np.ascontiguousarray(np.asarray(inputs["x"], dtype=np.float32))
    B, S, _ = x.shape
    NT = S // TT
    cores = list(range(B))
    if FUSED:
        nc = build(NT, (0, 1, 2, 3))
        wm = _wmap(inputs, (0, 1, 2, 3))
        in_maps = [dict(wm, xin=x[b]) for b in range(B)]
        res = run_bass_kernel_spmd(nc, in_maps, core_ids=cores)
        return np.stack([np.asarray(r["xout"]) for r in res.results], axis=0).astype(np.float32)
    cur = [x[b] for b in range(B)]
    kv = None
    for l in range(4):
        nc = build(NT, (l,), kv_in=(l >= 2), kv_out=(l == 1))
        wm = _wmap(inputs, (l,))
        in_maps = []
        for b in range(B):
            m = dict(wm, xin=cur[b])
            if l >= 2:
                m["kTd"] = kv[b][0]
                m["Vd"] = kv[b][1]
            in_maps.append(m)
        res = run_bass_kernel_spmd(nc, in_maps, core_ids=cores)
        cur = [np.asarray(r["xout"]) for r in res.results]
        if l == 1:
            kv = [(np.asarray(r["kTd"]), np.asarray(r["Vd"])) for r in res.results]
    return np.stack(cur, axis=0).astype(np.float32)
```

```python
import numpy as np
from contextlib import ExitStack
import concourse.bass as bass
import concourse.mybir as mybir
from concourse.bass_utils import run_bass_kernel_spmd

F32 = mybir.dt.float32
BF16 = mybir.dt.bfloat16
AF = mybir.ActivationFunctionType
ALU = mybir.AluOpType

D = 1024
DFF = 4096
TT = 512
ALPHA = float(8 ** 0.25)
LN_EPS = 1e-5
NEG_BIG = -30000.0
N_A = 2
LOOKAHEAD = 3
NSPLIT = 2
GATE_ENG = "pool"
ADDB_ENG = "dve"
NWBUF = 4


class Tok:
    __slots__ = ("sem", "value", "eng")

    def __init__(self, sem, eng, value=None):
        self.sem = sem
        self.eng = eng
        self.value = value


class Prog:
    ENG = ("pe", "act", "dve", "pool", "sp")

    def __init__(self, nc, es):
        self.nc = nc
        self.es = es
        self.lists = {e: [] for e in self.ENG}
        self.sems = {}
        for e in ("pe", "act", "dve", "pool"):
            self.sems["c_" + e] = es.enter_context(nc.semaphore("c_" + e))
        self.cnt = {e: 0 for e in self.ENG}
        self.cur = {e: Tok("c_" + e, e) for e in ("pe", "act", "dve", "pool")}
        self.waited = {}
        self.lastw = {}
        self.readers = {}
        self.dcnt = {}
        self.ninstr = 0

    def dsem(self, name):
        if name not in self.sems:
            self.sems[name] = self.es.enter_context(self.nc.semaphore(name))
            self.dcnt[name] = 0
        return name

    def _deps(self, eng, reads, writes):
        toks = []
        for k in reads:
            t = self.lastw.get(k)
            if t is not None:
                toks.append(t)
        for k in writes:
            t = self.lastw.get(k)
            if t is not None:
                toks.append(t)
            toks.extend(self.readers.get(k, ()))
        need = {}
        for t in toks:
            if t.eng == "pe" and eng == "pe":
                continue
            if t.value is None:
                raise RuntimeError(f"dependency on unsignaled op of {t.eng} from {eng}")
            if self.waited.get((eng, t.sem), 0) >= t.value:
                continue
            if need.get(t.sem, 0) < t.value:
                need[t.sem] = t.value
        for s, v in need.items():
            self.waited[(eng, s)] = v
        return list(need.items())

    def _record(self, tok, reads, writes):
        for k in reads:
            self.readers.setdefault(k, []).append(tok)
        for k in writes:
            self.lastw[k] = tok
            self.readers[k] = []

    def op(self, eng, fn, reads=(), writes=(), signal=True):
        waits = self._deps(eng, reads, writes)
        tok = self.cur[eng]
        if signal:
            self.cnt[eng] += 1
            tok.value = self.cnt[eng]
            self.cur[eng] = Tok("c_" + eng, eng)
        self.lists[eng].append((waits, fn, ("c_" + eng, 1) if signal else None))
        self._record(tok, reads, writes)
        self.ninstr += 1
        return tok

    def dma(self, eng, fn, sem, reads=(), writes=()):
        self.dsem(sem)
        waits = self._deps(eng, reads, writes)
        self.dcnt[sem] += 16
        tok = Tok(sem, "dma", self.dcnt[sem])
        self.lists[eng].append((waits, fn, (sem, 16)))
        self._record(tok, reads, writes)
        self.ninstr += 1
        return tok

    def wait_tok(self, eng, tok):
        if self.waited.get((eng, tok.sem), 0) >= tok.value:
            return
        self.waited[(eng, tok.sem)] = tok.value
        self.lists[eng].append(([(tok.sem, tok.value)], None, None))

    def emit(self, block):
        nc = self.nc
        engmap = {"pe": block.tensor, "act": block.scalar, "dve": block.vector, "pool": block.gpsimd, "sp": block.sync}
        for ename, deco in engmap.items():
            lst = self.lists[ename]
            sems = self.sems

            def body(e, lst=lst):
                for waits, fn, inc in lst:
                    for s, v in waits:
                        e.wait_ge(sems[s], v)
                    if fn is None:
                        continue
                    ins = fn(e)
                    if inc is not None:
                        ins.then_inc(sems[inc[0]], inc[1])
            deco(body)


def build(NT, layers=(0, 1, 2, 3), kv_in=False, kv_out=False):
    S = NT * TT
    layers = tuple(layers)
    need_kv_proj = (1 in layers)
    has_b = any(l >= N_A for l in layers)
    has_a = any(l < N_A for l in layers)
    nc = bass.Bass("TRN2", target_bir_lowering=False)

    def din(name, shape, dt=F32):
        return nc.dram_tensor(name, list(shape), dt, kind="ExternalInput").ap()

    xin = din("xin", [S, D])
    xout = nc.dram_tensor("xout", [S, D], F32, kind="ExternalOutput").ap()
    Wd = {}
    Wd["mix_ln_g"] = din("mix_ln_g", [4, D]); Wd["mix_ln_b"] = din("mix_ln_b", [4, D])
    Wd["ffn_ln_g"] = din("ffn_ln_g", [4, D]); Wd["ffn_ln_b"] = din("ffn_ln_b", [4, D])
    Wd["ffn_w1"] = din("ffn_w1", [4, D, DFF]); Wd["ffn_w2"] = din("ffn_w2", [4, DFF, D])
    if has_a:
        Wd["a_w_in"] = din("a_w_in", [2, D, 2 * D]); Wd["a_ln_g"] = din("a_ln_g", [2, D]); Wd["a_ln_b"] = din("a_ln_b", [2, D])
        Wd["a_w_s"] = din("a_w_s", [2, 8, 128, 128]); Wd["a_b_s"] = din("a_b_s", [2, 8 * 128]); Wd["a_w_out"] = din("a_w_out", [2, D, D])
    if need_kv_proj:
        Wd["sb_w_k"] = din("sb_w_k", [D, D]); Wd["sb_w_v"] = din("sb_w_v", [D, D])
    if has_b:
        Wd["b_w_q"] = din("b_w_q", [2, D, D]); Wd["b_w_o"] = din("b_w_o", [2, D, D])
    if need_kv_proj or has_b:
        kvkind = "ExternalInput" if kv_in else ("ExternalOutput" if kv_out else "Internal")
        kTd = nc.dram_tensor("kTd", [8, 128, S], BF16, kind=kvkind).ap()
        Vd = nc.dram_tensor("Vd", [S, D], BF16, kind=kvkind).ap()

    es = ExitStack()
    with es:
        def sb(name, shape, dt):
            return es.enter_context(nc.sbuf_tensor(name, list(shape), dt))

        ident = sb("ident", [128, 128], BF16)
        negtri = sb("negtri", [128, 128], BF16)
        negones = sb("negones", [128, 128], BF16)
        maskd = sb("maskd", [128, 4, 512], BF16)
        wsT = sb("wsT", [128, 2, 8, 128], BF16)
        bsb = sb("bsb", [128, 2, 8, 128], F32)
        x = sb("x", [128, 4, D], F32)
        xb = sb("xb", [128, 4, D], BF16)
        xT = sb("xT", [128, 8, TT], BF16)
        gbc = sb("gbc", [128, 2, 2, D], F32)
        uT = sb("uT", [128, 8, TT], BF16)
        vf = sb("vf", [128, 4, D], F32)
        vb = sb("vb", [128, 4, D], BF16)
        R1 = sb("R1", [128, 32 * TT], BF16)
        ebuf = sb("ebuf", [128, 2, 2, TT], F32)
        lbuf = sb("lbuf", [128, 2, 2, TT], BF16)
        lsum = sb("lsum", [128, 3, 2, TT], BF16)
        abuf = sb("abuf", [128, 2, 2, TT], BF16)
        qT = sb("qT", [128, 8, TT], BF16)
        oT = sb("oT", [128, 8, TT], BF16)
        wbuf = sb("wbuf", [128, NWBUF, 8, TT], BF16)
        st = sb("st", [128, 4, 12], F32)
        mv = sb("mv", [128, 4, 2], F32)
        lnv = sb("lnv", [128, 4], F32)
        rstd = sb("rstd", [128, 4], F32)
        nmr = sb("nmr", [128, 4], F32)
        PS = es.enter_context(nc.psum_tensor("PS", [128, 8, 512], F32))
        pT = PS[:, 7, :].bitcast(BF16).rearrange("p (a t) -> p a t", a=2)

        P = Prog(nc, es)
        cf = ebuf[:, :, :, :].rearrange("p a b t -> p (a b) t")
        wsf = vf[:, 0, :].rearrange("p (g t) -> p g t", t=128)
        wsb = vb[:, 0, :].rearrange("p (g t) -> p g t", t=128)

        def hT(ffc):
            return R1[:, ffc * TT:(ffc + 1) * TT]

        def kTb(slot):
            return R1[:, slot * 4096:(slot + 1) * 4096]

        def Vb(slot, kb):
            o = 8192 + slot * 4096 + kb * 128
            return R1[:, o:o + 128]

        def r1keys_h(ffc):
            return ["R1_%d" % ffc]

        def r1keys_kT(slot):
            return ["R1_%d" % i for i in range(slot * 8, slot * 8 + 8)]

        def r1keys_V(slot):
            return ["R1_%d" % i for i in range(16 + slot * 8, 16 + slot * 8 + 8)]

        P.op("pool", lambda e: e.memset(cf[:, 0, 0:128], 0.0), writes=["ebuf0"])
        P.op("pool", lambda e: e.affine_select(out=cf[:, 0, 0:128], in_=cf[:, 0, 0:128], pattern=[[-1, 128]],
                                                compare_op=ALU.not_equal, fill=1.0, base=0, channel_multiplier=1),
             reads=["ebuf0"], writes=["ebuf0"])
        P.op("dve", lambda e: e.tensor_copy(ident[:], cf[:, 0, 0:128]), reads=["ebuf0"], writes=["ident"])
        if has_b:
            P.op("pool", lambda e: e.memset(cf[:, 1, 0:128], -1.0), reads=["ebuf0"], writes=["ebuf0"])
            P.op("pool", lambda e: e.affine_select(out=cf[:, 1, 0:128], in_=cf[:, 1, 0:128], pattern=[[-1, 128]],
                                                    compare_op=ALU.is_ge, fill=0.0, base=0, channel_multiplier=1),
                 reads=["ebuf0"], writes=["ebuf0"])
            P.op("dve", lambda e: e.tensor_copy(negtri[:], cf[:, 1, 0:128]), reads=["ebuf0"], writes=["negtri"])
            P.op("dve", lambda e: e.memset(negones[:], -1.0), writes=["negones"])
            P.op("pool", lambda e: e.memset(cf[:, :, :], NEG_BIG), reads=["ebuf0", "ebuf1"], writes=["ebuf0", "ebuf1"])
            P.op("pool", lambda e: e.affine_select(out=cf[:, :, :], in_=cf[:, :, :], pattern=[[128, 4], [-1, 512]],
                                                    compare_op=ALU.is_ge, fill=0.0, base=0, channel_multiplier=1),
                 reads=["ebuf0", "ebuf1"], writes=["ebuf0", "ebuf1"])
            P.op("dve", lambda e: e.tensor_copy(maskd[:, :, :], cf[:, :, :]), reads=["ebuf0", "ebuf1"], writes=["maskd"])
        if has_a:
            for l in range(2):
                P.dma("sp", lambda e, l=l: e.dma_start(out=wsf[:, :, :], in_=Wd["a_w_s"][l].rearrange("g t s -> t g s")),
                      "cst_ws%d" % l, writes=["bB0"])
                P.op("dve", lambda e: e.tensor_copy(wsb[:, :, :], wsf[:, :, :]), reads=["bB0"], writes=["vb0"])
                for g in range(8):
                    P.op("pe", lambda e, g=g: e.transpose(pT[:, g // 4, (g % 4) * 128:(g % 4 + 1) * 128], wsb[:, g, :], ident[:]),
                         reads=["vb0", "ident"], writes=["ps7"], signal=(g == 7))
                P.op("dve", lambda e, l=l: e.tensor_copy(wsT[:, l, :, :], pT[:, :, :].rearrange("p a (q t) -> p (a q) t", t=128)), reads=["ps7"], writes=["wsT"])
                P.op("dve", lambda e, l=l: e.memset(wsT[64:128, l, :, 0:64], 0.0), reads=["wsT"], writes=["wsT"])
                P.dma("sp", lambda e, l=l: e.dma_start(out=bsb[:, l, :, :].rearrange("p g t -> p (g t)"), in_=Wd["a_b_s"][l:l + 1, :].broadcast_to([128, 1024])),
                      "cst_bs%d" % l, writes=["bsb"])

        def piece_src(p):
            kind = p[0]
            if kind == "win_u":
                return Wd["a_w_in"][p[1]], 0, p[2] * 512
            if kind == "win_v":
                return Wd["a_w_in"][p[1]], 0, 1024 + p[2] * 512
            if kind == "wout":
                return Wd["a_w_out"][p[1]], 0, p[2] * 512
            if kind == "w1":
                return Wd["ffn_w1"][p[1]], 0, p[2] * 512
            if kind == "w2":
                return Wd["ffn_w2"][p[1]], p[3] * 8, p[2] * 512
            if kind == "wk":
                return Wd["sb_w_k"], 0, p[1] * 512
            if kind == "wv":
                return Wd["sb_w_v"], 0, p[1] * 512
            if kind == "wq":
                return Wd["b_w_q"][p[1]], 0, p[2] * 512
            if kind == "wo":
                return Wd["b_w_o"][p[1]], 0, p[2] * 512
            raise KeyError(kind)

        def pieces_ffn(l):
            return [("w1", l, g) for g in range(8)] + [("w2", l, h, g) for h in range(2) for g in range(4)]

        def pieces_tile():
            out = []
            for l in layers:
                if l < N_A:
                    out += [("win_v", l, 0), ("win_v", l, 1), ("win_u", l, 0), ("win_u", l, 1), ("wout", l, 0), ("wout", l, 1)]
                    out += pieces_ffn(l)
                    if l == 1:
                        out += [("wk", 0), ("wk", 1), ("wv", 0), ("wv", 1)]
                else:
                    j = l - N_A
                    out += [("wq", j, 0), ("wq", j, 1), ("wo", j, 0), ("wo", j, 1)]
                    out += pieces_ffn(l)
            return out

        sched = []
        for J in range(NT):
            sched += pieces_tile()
        wstate = {"issued": 0, "next": 0}

        def issue_piece(i):
            p = sched[i]
            slot = i % NWBUF
            W2, r0, c0 = piece_src(p)
            src = W2[r0 * 128:(r0 + 8) * 128, c0:c0 + 512].rearrange("(k p) c -> p k c", p=128)
            P.dma("pool", lambda e, src=src, slot=slot: e.dma_start(out=wbuf[:, slot, :, :], in_=src),
                  "w%d" % slot, writes=["wbuf%d" % slot])

        def get_piece(expect, keep_prev=False):
            i = wstate["next"]
            assert sched[i] == expect, (sched[i], expect)
            released = i - (1 if keep_prev else 0)
            while (wstate["issued"] < len(sched) and wstate["issued"] < i + 1 + LOOKAHEAD
                   and wstate["issued"] - NWBUF < released):
                issue_piece(wstate["issued"])
                wstate["issued"] += 1
            assert wstate["issued"] > i
            wstate["next"] += 1
            slot = i % NWBUF
            return slot, "wbuf%d" % slot

        rot = {"i": 0}

        def rotbank():
            b = 4 + rot["i"] % 3
            rot["i"] += 1
            return b

        gbstate = {"i": 0}

        def load_gb(gname, bname, l):
            slot = gbstate["i"] % 2
            gbstate["i"] += 1
            key = "gb%d" % slot
            P.dma("sp", lambda e: e.dma_start(out=gbc[:, slot, 0, :], in_=Wd[gname][l:l + 1, :].broadcast_to([128, D])),
                  "gbsg%d" % slot, writes=[key])
            P.dma("sp", lambda e: e.dma_start(out=gbc[:, slot, 1, :], in_=Wd[bname][l:l + 1, :].broadcast_to([128, D])),
                  "gbsb%d" % slot, writes=[key + "b"])
            return slot

        XT_KEYS = ["xTs%d" % s_ for s_ in range(4)]
        trc = {"i": 0}

        def tr_sub(s):
            b = rotbank()
            tb = PS[:, b, :].bitcast(BF16).rearrange("p (k t) -> p k t", t=128)
            for k in range(8):
                P.op("pe", lambda e, s=s, k=k, tb=tb: e.transpose(tb[:, k, :], xb[:, s, k * 128:(k + 1) * 128], ident[:]),
                     reads=["xb%d" % s, "ident"], writes=["ps%d" % b], signal=(k == 7))
            P.op("act", lambda e, s=s, tb=tb: e.copy(xT[:, :, s * 128:(s + 1) * 128], tb[:, :, :]), reads=["ps%d" % b], writes=["xTs%d" % s])

        class LNPipe:
            def __init__(self, stages):
                self.stages = stages
                self.i = 0

            def step(self):
                i = self.i
                for k_, stage in enumerate(self.stages):
                    s_ = i - k_
                    if 0 <= s_ < 4:
                        stage(s_)
                self.i += 1

            def finish(self):
                while self.i < 4 + len(self.stages) - 1:
                    self.step()

        def layer_norm(buf, bkey, gslot, out_bf, okey, with_f32=True, do_xT=True, need_bf=True, tail_stage=None, early_h0=True):
            gk, bk = "gb%d" % gslot, "gb%db" % gslot

            def stats(s):
                if not early_h0:
                    P.op("dve", lambda e: e.bn_stats(st[:, s, 0:6], buf[:, s, 0:512]), reads=[bkey % s], writes=["st%d" % s])
                P.op("dve", lambda e: e.bn_stats(st[:, s, 6:12], buf[:, s, 512:1024]), reads=[bkey % s], writes=["st%d" % s])
                P.op("dve", lambda e: e.bn_aggr(mv[:, s, :], st[:, s, :]), reads=["st%d" % s], writes=["mv%d" % s])

            def rs(s):
                P.op("act", lambda e: e.activation(out=lnv[:, s:s + 1], in_=mv[:, s, 1:2], func=AF.Ln, bias=LN_EPS), reads=["mv%d" % s], writes=["lnv%d" % s])
                P.op("act", lambda e: e.activation(out=rstd[:, s:s + 1], in_=lnv[:, s:s + 1], func=AF.Exp, scale=-0.5), reads=["lnv%d" % s], writes=["rstd%d" % s])

            def opA(s):
                P.op("dve", lambda e: e.scalar_tensor_tensor(out=buf[:, s, :], in0=buf[:, s, :], scalar=mv[:, s, 0:1], in1=gbc[:, gslot, 0, :],
                                                             op0=ALU.subtract, op1=ALU.mult),
                     reads=[bkey % s, "mv%d" % s, gk], writes=[bkey % s])

            def opB(s):
                if with_f32:
                    P.op("dve", lambda e: e.scalar_tensor_tensor(out=buf[:, s, :], in0=buf[:, s, :], scalar=rstd[:, s:s + 1], in1=gbc[:, gslot, 1, :],
                                                                 op0=ALU.mult, op1=ALU.add),
                         reads=[bkey % s, "rstd%d" % s, bk], writes=[bkey % s])
                else:
                    P.op("dve", lambda e: e.scalar_tensor_tensor(out=out_bf[:, s, :], in0=buf[:, s, :], scalar=rstd[:, s:s + 1], in1=gbc[:, gslot, 1, :],
                                                                 op0=ALU.mult, op1=ALU.add),
                         reads=[bkey % s, "rstd%d" % s, bk], writes=[okey % s])

            def cpy(s):
                P.op("act", lambda e: e.copy(out_bf[:, s, :], buf[:, s, :]), reads=[bkey % s], writes=[okey % s])

            stages = [stats, rs, opA, opB]
            if with_f32 and need_bf:
                stages.append(cpy)
                if do_xT:
                    stages.append(tr_sub)
            if tail_stage is not None:
                stages.append(tail_stage)
            return LNPipe(stages)

        def fm_piece(pi, slot, wkey, srcT, evac_j, split):
            if split:
                banks = [0, 1, 2, 3] if pi % 2 == 0 else [4, 5, 6, 7]
                for hf in range(2):
                    rk = ["xTs%d" % (2 * hf), "xTs%d" % (2 * hf + 1)]
                    for j in range(4):
                        b = banks[j]
                        for k in range(8):
                            P.op("pe", lambda e, j=j, k=k, b=b, hf=hf: e.matmul(PS[:, b, hf * 256:(hf + 1) * 256], wbuf[:, slot, k, j * 128:(j + 1) * 128],
                                                                            srcT[:, k, hf * 256:(hf + 1) * 256], start=(k == 0), stop=(k == 7)),
                                 reads=[wkey] + rk, writes=["ps%d" % b], signal=(k == 7))
                        if hf == 1:
                            evac_j(j, b)
            else:
                for j in range(4):
                    b = rotbank()
                    for k in range(8):
                        P.op("pe", lambda e, j=j, k=k, b=b: e.matmul(PS[:, b, :], wbuf[:, slot, k, j * 128:(j + 1) * 128], srcT[:, k, :], start=(k == 0), stop=(k == 7)),
                             reads=[wkey] + XT_KEYS, writes=["ps%d" % b], signal=(k == 7))
                    evac_j(j, b)

        def gemm_fm(pieces, srcT, src_keys, evac, nsplit=0):
            for pi, pc in enumerate(pieces):
                slot, wkey = get_piece(pc)
                fm_piece(pi, slot, wkey, srcT, (lambda j, b, pi=pi: evac(pi * 4 + j, b)), split=(pi < nsplit))

        def gemm_tm(pieces, srcT, src_keys, evac, after_s=None):
            slots = [get_piece(pieces[0]), get_piece(pieces[1], keep_prev=True)]
            for s in range(4):
                for half in range(2):
                    slot, wkey = slots[half]
                    b = (2 * s + half) % 4
                    for k in range(8):
                        P.op("pe", lambda e, slot=slot, s=s, k=k, b=b: e.matmul(PS[:, b, :], srcT[:, k, s * 128:(s + 1) * 128], wbuf[:, slot, k, :], start=(k == 0), stop=(k == 7)),
                             reads=[wkey] + src_keys(s, k), writes=["ps%d" % b], signal=(k == 7))
                    evac(s, half, b)
                if after_s is not None:
                    after_s(s)

        cur = {"x": x, "xk": "bA%d", "v": vf, "vk": "bB%d"}

        def evac_resid(s, half, b):
            xx, xk = cur["x"], cur["xk"]
            P.op("dve", lambda e: e.scalar_tensor_tensor(out=xx[:, s, half * 512:(half + 1) * 512], in0=xx[:, s, half * 512:(half + 1) * 512], scalar=ALPHA,
                                                         in1=PS[:, b, :], op0=ALU.mult, op1=ALU.add),
                 reads=[xk % s, "ps%d" % b], writes=[xk % s])
            if half == 0:
                P.op("dve", lambda e: e.bn_stats(st[:, s, 0:6], xx[:, s, 0:512]), reads=[xk % s], writes=["st%d" % s])

        def ffn(l, need_xT=True, mid_hook=None, tail_stage=None):
            gslot = load_gb("ffn_ln_g", "ffn_ln_b", l)
            def evac_h(g, j, b):
                ffc = g * 4 + j
                es_ = ffc % 2
                P.op("act", lambda e: e.activation(out=ebuf[:, es_, 0, :], in_=PS[:, b, :], func=AF.Relu),
                     reads=["ps%d" % b], writes=["ebuf%d" % es_])
                P.op("dve", lambda e: e.tensor_tensor(out=hT(ffc), in0=ebuf[:, es_, 0, :], in1=ebuf[:, es_, 0, :], op=ALU.mult),
                     reads=["ebuf%d" % es_], writes=r1keys_h(ffc))
            for g in range(8):
                slot, wkey = get_piece(("w1", l, g))
                fm_piece(g, slot, wkey, xT, (lambda j, b, g=g: evac_h(g, j, b)), split=(g < NSPLIT))
            if mid_hook is not None:
                mid_hook()
            lnp = layer_norm(cur["x"], cur["xk"], gslot, xb, "xb%d", do_xT=need_xT, need_bf=need_xT, tail_stage=tail_stage)
            for half in range(2):
                for g8 in range(4):
                    slot, wkey = get_piece(("w2", l, half, g8))
                    for s in range(4):
                        for c in range(8):
                            ffc = g8 * 8 + c
                            yb = s + 4 * half
                            P.op("pe", lambda e, slot=slot, s=s, c=c, ffc=ffc, g8=g8, yb=yb: e.matmul(PS[:, yb, :], hT(ffc)[:, s * 128:(s + 1) * 128], wbuf[:, slot, c, :],
                                                                                                 start=(g8 == 0 and c == 0), stop=(g8 == 3 and c == 7)),
                                 reads=[wkey] + r1keys_h(ffc), writes=["ps%d" % yb], signal=(c == 7))
                        if g8 == 3:
                            evac_resid(s, half, s + 4 * half)
                            if half == 1:
                                lnp.step()
            lnp.finish()

        def gmlp(l):
            gslot_v = load_gb("a_ln_g", "a_ln_b", l)
            gslot_m = load_gb("mix_ln_g", "mix_ln_b", l)

            vv, vk = cur["v"], cur["vk"]

            def evac_v(s, half, b):
                P.op("act", lambda e: e.activation(out=vv[:, s, half * 512:(half + 1) * 512], in_=PS[:, b, :], func=AF.Gelu_apprx_tanh),
                     reads=["ps%d" % b], writes=[vk % s])
                if half == 0:
                    P.op("dve", lambda e: e.bn_stats(st[:, s, 0:6], vv[:, s, 0:512]), reads=[vk % s], writes=["st%d" % s])
            lnv_ = layer_norm(vv, vk, gslot_v, vb, "vb%d", with_f32=False)
            gemm_tm([("win_v", l, 0), ("win_v", l, 1)], xT, lambda s_, k_: ["xTs%d" % s_], evac_v, after_s=lambda s_: lnv_.step())

            def evac_u(oc, b):
                P.op("act", lambda e: e.activation(out=uT[:, oc, :], in_=PS[:, b, :], func=AF.Gelu_apprx_tanh),
                     reads=["ps%d" % b], writes=["uT%d" % oc])
            lnv_.finish()
            gemm_fm([("win_u", l, 0), ("win_u", l, 1)], xT, XT_KEYS, evac_u)
            for g in range(8):
                b = g
                for s in range(4):
                    P.op("pe", lambda e, g=g, s=s, b=b: e.matmul(PS[:, b, s * 128:(s + 1) * 128], vb[:, s, g * 128:(g + 1) * 128], wsT[:, l, g, :], start=True, stop=True),
                         reads=["vb%d" % s, "wsT"], writes=["ps%d" % b], signal=(s == 3))
                es_ = g % 2
                P.op("dve", lambda e, g=g, b=b, es_=es_: e.tensor_tensor(out=ebuf[:, es_, 0, :].rearrange("p (s t) -> p s t", t=128),
                                                                       in0=PS[:, b, :].rearrange("p (s t) -> p s t", t=128),
                                                                       in1=bsb[:, l, g:g + 1, :].broadcast_to([128, 4, 128]), op=ALU.add),
                     reads=["ps%d" % b, "bsb"], writes=["ebuf%d" % es_])
                P.op(GATE_ENG, lambda e, g=g, es_=es_: e.tensor_tensor(out=uT[:, g, :], in0=ebuf[:, es_, 0, :], in1=uT[:, g, :], op=ALU.mult),
                     reads=["ebuf%d" % es_, "uT%d" % g], writes=["uT%d" % g])
            lnm_ = layer_norm(cur["x"], cur["xk"], gslot_m, xb, "xb%d")
            gemm_tm([("wout", l, 0), ("wout", l, 1)], uT, lambda s_, k_: ["uT%d" % k_], evac_resid, after_s=lambda s_: lnm_.step())
            lnm_.finish()

        def kv_proj(J):
            def evac_k(oc, b):
                P.op("act", lambda e: e.copy(uT[:, oc, :], PS[:, b, :]), reads=["ps%d" % b], writes=["uT%d" % oc])
            gemm_fm([("wk", 0), ("wk", 1)], xT, XT_KEYS, evac_k, nsplit=NSPLIT)

            def evac_vv(s, half, b):
                P.op("dve", lambda e: e.tensor_copy(vb[:, s, half * 512:(half + 1) * 512], PS[:, b, :]), reads=["ps%d" % b], writes=["vb%d" % s])
            gemm_tm([("wv", 0), ("wv", 1)], xT, lambda s_, k_: ["xTs%d" % s_], evac_vv)
            P.dma("sp", lambda e: e.dma_start(out=kTd.rearrange("c p s -> p c s")[:, :, J * TT:(J + 1) * TT], in_=uT[:, :, :]),
                  "kst", reads=["uT%d" % k for k in range(8)], writes=["kd%d" % J])
            P.dma("sp", lambda e: e.dma_start(out=Vd[J * TT:(J + 1) * TT, :].rearrange("(s p) d -> p s d", p=128), in_=vb[:, :, :]),
                  "vst", reads=["vb%d" % s for s in range(4)], writes=["vd%d" % J])

        def load_kv_chunk(J, c):
            slot = c % 2
            n = (J + 1) * TT
            nkb = 4 * (J + 1)
            P.dma("sp", lambda e: e.dma_start(out=kTb(slot)[:, 0:n], in_=kTd[c][:, 0:n]),
                  "kl%d" % slot, reads=["kd%d" % t for t in range(J + 1)], writes=r1keys_kT(slot))
            vdst = R1[:, 8192 + slot * 4096: 8192 + slot * 4096 + nkb * 128].rearrange("p (b d) -> p b d", d=128)
            P.dma("sp", lambda e: e.dma_start(out=vdst, in_=Vd[0:n, c * 128:(c + 1) * 128].rearrange("(b p) d -> p b d", p=128)),
                  "vl%d" % slot, reads=["vd%d" % t for t in range(J + 1)], writes=r1keys_V(slot))

        def attention(J):
            nkb = 4 * (J + 1)
            U = nkb
            G = 8 * U

            def info(g):
                c, u = divmod(g, U)
                kb = nkb - 1 - u
                i = kb - 4 * J
                return c, u, kb, i, (i * 128 if i > 0 else 0)

            def Zop(g):
                c, u, kb, i, c0 = info(g)
                slot = c % 2
                zb = 2 * (g % 3)
                for h in range(2):
                    P.op("pe", lambda e, h=h: e.matmul(PS[:, zb + h, c0:512], kTb(slot)[h * 64:(h + 1) * 64, kb * 128:(kb + 1) * 128], qT[h * 64:(h + 1) * 64, c, c0:512],
                                                       start=True, stop=(i < 0)),
                         reads=r1keys_kT(slot) + ["qT%d" % c], writes=["ps%d" % (zb + h)], signal=(h == 1 and i < 0))
                if i >= 0:
                    for h in range(2):
                        P.op("pe", lambda e, h=h: e.matmul(PS[:, zb + h, c0:512], ident[:], maskd[:, i, c0:512], start=False, stop=True),
                             reads=["ident", "maskd"], writes=["ps%d" % (zb + h)], signal=(h == 1))

            def TRIop(g):
                c, u, kb, i, c0 = info(g)
                zb = 2 * (g % 3)
                for h in range(2):
                    P.op("pe", lambda e, h=h: e.matmul(PS[:, zb + h, c0:512], negtri[:], lbuf[:, g % 2, h, c0:512], start=False, stop=True, skip_group_check=True),
                         reads=["negtri", "lbuf%d" % (g % 2)], writes=["ps%d" % (zb + h)], signal=(u == 0 and h == 1))
                if u > 0:
                    pc0 = info(g - 1)[4]
                    for h in range(2):
                        P.op("pe", lambda e, h=h: e.matmul(PS[:, zb + h, pc0:512], negones[:], lsum[:, (g - 1) % 3, h, pc0:512], start=False, stop=True, skip_group_check=True),
                             reads=["negones", "lsum%d" % ((g - 1) % 3)], writes=["ps%d" % (zb + h)], signal=(h == 1))

            def AVop(g):
                c, u, kb, i, c0 = info(g)
                slot = c % 2
                ob = 6 + (c % 2)
                for h in range(2):
                    P.op("pe", lambda e, h=h: e.matmul(PS[h * 64:(h + 1) * 64, ob, c0:512], Vb(slot, kb)[:, h * 64:(h + 1) * 64], abuf[:, g % 2, h, c0:512],
                                                       start=(u == 0), stop=(u == U - 1), skip_group_check=True),
                         reads=r1keys_V(slot) + ["abuf%d" % (g % 2)], writes=["ps%d" % ob], signal=(h == 1))
                if u == U - 1:
                    P.op("dve", lambda e: e.tensor_copy(oT[:, c, :], PS[:, ob, :]), reads=["ps%d" % ob], writes=["oT%d" % c])
                    if c + 2 < 8:
                        load_kv_chunk(J, c + 2)

            def E1L(g):
                c, u, kb, i, c0 = info(g)
                zb = 2 * (g % 3)
                P.op("act", lambda e: e.activation(out=ebuf[:, g % 2, :, c0:512], in_=PS[:, zb:zb + 2, c0:512], func=AF.Exp),
                     reads=["ps%d" % zb, "ps%d" % (zb + 1)], writes=["ebuf%d" % (g % 2)])
                P.op("act", lambda e: e.activation(out=lbuf[:, g % 2, :, c0:512], in_=ebuf[:, g % 2, :, c0:512], func=AF.Ln, bias=1.0),
                     reads=["ebuf%d" % (g % 2)], writes=["lbuf%d" % (g % 2)])

            def E2(g):
                c, u, kb, i, c0 = info(g)
                zb = 2 * (g % 3)
                P.op("act", lambda e: e.activation(out=abuf[:, g % 2, :, c0:512], in_=PS[:, zb:zb + 2, c0:512], func=AF.Exp),
                     reads=["ps%d" % zb, "ps%d" % (zb + 1)], writes=["abuf%d" % (g % 2)])

            def LS(g):
                c, u, kb, i, c0 = info(g)
                if u + 1 >= U:
                    return
                cur, prv = g % 3, (g - 1) % 3
                if u == 0:
                    P.op("dve", lambda e: e.tensor_copy(lsum[:, cur, :, c0:512], lbuf[:, g % 2, :, c0:512]), reads=["lbuf%d" % (g % 2)], writes=["lsum%d" % cur])
                else:
                    pc0 = info(g - 1)[4]
                    P.op("dve", lambda e: e.tensor_tensor(out=lsum[:, cur, :, pc0:512], in0=lsum[:, prv, :, pc0:512], in1=lbuf[:, g % 2, :, pc0:512], op=ALU.add),
                         reads=["lsum%d" % prv, "lbuf%d" % (g % 2)], writes=["lsum%d" % cur])
                    if pc0 > c0:
                        P.op("dve", lambda e: e.tensor_copy(lsum[:, cur, :, c0:pc0], lbuf[:, g % 2, :, c0:pc0]), reads=["lbuf%d" % (g % 2)], writes=["lsum%d" % cur])

            load_kv_chunk(J, 0)
            load_kv_chunk(J, 1)
            Zop(0)
            for step in range(G + 2):
                if 0 <= step - 1 < G:
                    TRIop(step - 1)
                if step + 1 < G:
                    Zop(step + 1)
                if 0 <= step - 2 < G:
                    AVop(step - 2)
                if step < G:
                    E1L(step)
                if 0 <= step - 1 < G:
                    E2(step - 1)
                if step < G:
                    LS(step)

        def sb_layer(l, J):
            j = l - N_A
            gslot_m = load_gb("mix_ln_g", "mix_ln_b", l)

            def evac_q(oc, b):
                P.op("act", lambda e: e.activation(out=qT[:, oc, :], in_=PS[:, b, :], func=AF.Identity, scale=0.125),
                     reads=["ps%d" % b], writes=["qT%d" % oc])
            gemm_fm([("wq", j, 0), ("wq", j, 1)], xT, XT_KEYS, evac_q, nsplit=NSPLIT)
            attention(J)
            lnm_ = layer_norm(cur["x"], cur["xk"], gslot_m, xb, "xb%d")
            gemm_tm([("wo", j, 0), ("wo", j, 1)], oT, lambda s_, k_: ["oT%d" % k_], evac_resid, after_s=lambda s_: lnm_.step())
            lnm_.finish()

        bufs = [(x, "bA%d"), (vf, "bB%d")]

        def load_x(J):
            dst, kf = bufs[J % 2]
            P.dma("sp", lambda e: e.dma_start(out=dst[:, :, :], in_=xin[J * TT:(J + 1) * TT, :].rearrange("(s p) d -> p s d", p=128)),
                  "xld", writes=[kf % s for s in range(4)])

        def prologue(J):
            src, kf = bufs[J % 2]
            for s in range(4):
                P.op("act", lambda e, s=s: e.copy(xb[:, s, :], src[:, s, :]), reads=[kf % s], writes=["xb%d" % s])
                tr_sub(s)

        a_layers = [l for l in layers if l < N_A]
        load_x(0)
        prologue(0)
        for J in range(NT):
            cur["x"], cur["xk"] = bufs[J % 2]
            cur["v"], cur["vk"] = bufs[(J + 1) % 2]
            nxt = (J + 1 < NT)
            if nxt and not a_layers:
                load_x(J + 1)
            hooked = False
            for l in layers:
                last = (l == layers[-1])
                hook = None
                if last and nxt and l != 1:
                    hook = (lambda J=J: prologue(J + 1))
                    hooked = True
                tail = None
                if last:
                    xs, xkf = cur["x"], cur["xk"]

                    def tail(s_, J=J, xs=xs, xkf=xkf):
                        P.dma("sp", lambda e: e.dma_start(out=xout[J * TT + s_ * 128:J * TT + (s_ + 1) * 128, :], in_=xs[:, s_, :]),
                              "xst%d" % s_, reads=[xkf % s_])
                if l < N_A:
                    gmlp(l)
                    if nxt and l == a_layers[-1]:
                        load_x(J + 1)
                    ffn(l, need_xT=(l == 1 or not last), mid_hook=hook, tail_stage=tail)
                    if l == 1:
                        kv_proj(J)
                else:
                    sb_layer(l, J)
                    ffn(l, need_xT=not last, mid_hook=hook, tail_stage=tail)
            if nxt and not hooked:
                prologue(J + 1)
        for s_ in range(4):
            P.wait_tok("sp", Tok("xst%d" % s_, "dma", P.dcnt["xst%d" % s_]))
        if need_kv_proj and kv_out:
            P.wait_tok("sp", Tok("kst", "dma", P.dcnt["kst"]))
            P.wait_tok("sp", Tok("vst", "dma", P.dcnt["vst"]))
        block = es.enter_context(nc.Block())
        P.emit(block)
    return nc


_WNAMES_ALL = ["a_w_in", "a_ln_g", "a_ln_b", "a_w_s", "a_b_s", "a_w_out", "sb_w_k", "sb_w_v", "b_w_q", "b_w_o",
               "mix_ln_g", "mix_ln_b", "ffn_ln_g", "ffn_ln_b", "ffn_w1", "ffn_w2"]


def _wmap(inputs, layers):
    layers = tuple(layers)
    has_a = any(l < N_A for l in layers)
    has_b = any(l >= N_A for l in layers)
    names = ["mix_ln_g", "mix_ln_b", "ffn_ln_g", "ffn_ln_b", "ffn_w1", "ffn_w2"]
    if has_a:
        names += ["a_w_in", "a_ln_g", "a_ln_b", "a_w_s", "a_b_s", "a_w_out"]
    if 1 in layers:
        names += ["sb_w_k", "sb_w_v"]
    if has_b:
        names += ["b_w_q", "b_w_o"]
    m = {}
    for n in names:
        a = np.ascontiguousarray(np.asarray(inputs[n], dtype=np.float32))
        if n == "a_b_s":
            a = a.reshape(2, 8 * 128)
        m[n] = a
    return m


FUSED = True


def kernel(**inputs):
    x = np.ascontiguousarray(np.asarray(inputs["x"], dtype=np.float32))
    B, S, _ = x.shape
    NT = S // TT
    cores = list(range(B))
    if FUSED:
        nc = build(NT, (0, 1, 2, 3))
        wm = _wmap(inputs, (0, 1, 2, 3))
        in_maps = [dict(wm, xin=x[b]) for b in range(B)]
        res = run_bass_kernel_spmd(nc, in_maps, core_ids=cores)
        return np.stack([np.asarray(r["xout"]) for r in res.results], axis=0).astype(np.float32)
    cur = [x[b] for b in range(B)]
    kv = None
    for l in range(4):
        nc = build(NT, (l,), kv_in=(l >= 2), kv_out=(l == 1))
        wm = _wmap(inputs, (l,))
        in_maps = []
        for b in range(B):
            m = dict(wm, xin=cur[b])
            if l >= 2:
                m["kTd"] = kv[b][0]
                m["Vd"] = kv[b][1]
            in_maps.append(m)
        res = run_bass_kernel_spmd(nc, in_maps, core_ids=cores)
        cur = [np.asarray(r["xout"]) for r in res.results]
        if l == 1:
            kv = [(np.asarray(r["kTd"]), np.asarray(r["Vd"])) for r in res.results]
    return np.stack(cur, axis=0).astype(np.float32)
```

```python
import numpy as np
from contextlib import ExitStack
import concourse.bass as bass
import concourse.mybir as mybir
from concourse.bass_utils import run_bass_kernel_spmd

F32 = mybir.dt.float32
BF16 = mybir.dt.bfloat16
AF = mybir.ActivationFunctionType
ALU = mybir.AluOpType

D = 1024
DFF = 4096
TT = 512
ALPHA = float(8 ** 0.25)
LN_EPS = 1e-5
NEG_BIG = -30000.0
N_A = 2
LOOKAHEAD = 3
NSPLIT = 2
GATE_ENG = "pool"
ADDB_ENG = "dve"
NWBUF = 4


class Tok:
    __slots__ = ("sem", "value", "eng")

    def __init__(self, sem, eng, value=None):
        self.sem = sem
        self.eng = eng
        self.value = value


class Prog:
    ENG = ("pe", "act", "dve", "pool", "sp")

    def __init__(self, nc, es):
        self.nc = nc
        self.es = es
        self.lists = {e: [] for e in self.ENG}
        self.sems = {}
        for e in ("pe", "act", "dve", "pool"):
            self.sems["c_" + e] = es.enter_context(nc.semaphore("c_" + e))
        self.cnt = {e: 0 for e in self.ENG}
        self.cur = {e: Tok("c_" + e, e) for e in ("pe", "act", "dve", "pool")}
        self.waited = {}
        self.lastw = {}
        self.readers = {}
        self.dcnt = {}
        self.ninstr = 0

    def dsem(self, name):
        if name not in self.sems:
            self.sems[name] = self.es.enter_context(self.nc.semaphore(name))
            self.dcnt[name] = 0
        return name

    def _deps(self, eng, reads, writes):
        toks = []
        for k in reads:
            t = self.lastw.get(k)
            if t is not None:
                toks.append(t)
        for k in writes:
            t = self.lastw.get(k)
            if t is not None:
                toks.append(t)
            toks.extend(self.readers.get(k, ()))
        need = {}
        for t in toks:
            if t.eng == "pe" and eng == "pe":
                continue
            if t.value is None:
                raise RuntimeError(f"dependency on unsignaled op of {t.eng} from {eng}")
            if self.waited.get((eng, t.sem), 0) >= t.value:
                continue
            if need.get(t.sem, 0) < t.value:
                need[t.sem] = t.value
        for s, v in need.items():
            self.waited[(eng, s)] = v
        return list(need.items())

    def _record(self, tok, reads, writes):
        for k in reads:
            self.readers.setdefault(k, []).append(tok)
        for k in writes:
            self.lastw[k] = tok
            self.readers[k] = []

    def op(self, eng, fn, reads=(), writes=(), signal=True):
        waits = self._deps(eng, reads, writes)
        tok = self.cur[eng]
        if signal:
            self.cnt[eng] += 1
            tok.value = self.cnt[eng]
            self.cur[eng] = Tok("c_" + eng, eng)
        self.lists[eng].append((waits, fn, ("c_" + eng, 1) if signal else None))
        self._record(tok, reads, writes)
        self.ninstr += 1
        return tok

    def dma(self, eng, fn, sem, reads=(), writes=()):
        self.dsem(sem)
        waits = self._deps(eng, reads, writes)
        self.dcnt[sem] += 16
        tok = Tok(sem, "dma", self.dcnt[sem])
        self.lists[eng].append((waits, fn, (sem, 16)))
        self._record(tok, reads, writes)
        self.ninstr += 1
        return tok

    def wait_tok(self, eng, tok):
        if self.waited.get((eng, tok.sem), 0) >= tok.value:
            return
        self.waited[(eng, tok.sem)] = tok.value
        self.lists[eng].append(([(tok.sem, tok.value)], None, None))

    def emit(self, block):
        nc = self.nc
        engmap = {"pe": block.tensor, "act": block.scalar, "dve": block.vector, "pool": block.gpsimd, "sp": block.sync}
        for ename, deco in engmap.items():
            lst = self.lists[ename]
            sems = self.sems

            def body(e, lst=lst):
                for waits, fn, inc in lst:
                    for s, v in waits:
                        e.wait_ge(sems[s], v)
                    if fn is None:
                        continue
                    ins = fn(e)
                    if inc is not None:
                        ins.then_inc(sems[inc[0]], inc[1])
            deco(body)


def build(NT, layers=(0, 1, 2, 3), kv_in=False, kv_out=False):
    S = NT * TT
    layers = tuple(layers)
    need_kv_proj = (1 in layers)
    has_b = any(l >= N_A for l in layers)
    has_a = any(l < N_A for l in layers)
    nc = bass.Bass("TRN2", target_bir_lowering=False)

    def din(name, shape, dt=F32):
        return nc.dram_tensor(name, list(shape), dt, kind="ExternalInput").ap()

    xin = din("xin", [S, D])
    xout = nc.dram_tensor("xout", [S, D], F32, kind="ExternalOutput").ap()
    Wd = {}
    Wd["mix_ln_g"] = din("mix_ln_g", [4, D]); Wd["mix_ln_b"] = din("mix_ln_b", [4, D])
    Wd["ffn_ln_g"] = din("ffn_ln_g", [4, D]); Wd["ffn_ln_b"] = din("ffn_ln_b", [4, D])
    Wd["ffn_w1"] = din("ffn_w1", [4, D, DFF]); Wd["ffn_w2"] = din("ffn_w2", [4, DFF, D])
    if has_a:
        Wd["a_w_in"] = din("a_w_in", [2, D, 2 * D]); Wd["a_ln_g"] = din("a_ln_g", [2, D]); Wd["a_ln_b"] = din("a_ln_b", [2, D])
        Wd["a_w_s"] = din("a_w_s", [2, 8, 128, 128]); Wd["a_b_s"] = din("a_b_s", [2, 8 * 128]); Wd["a_w_out"] = din("a_w_out", [2, D, D])
    if need_kv_proj:
        Wd["sb_w_k"] = din("sb_w_k", [D, D]); Wd["sb_w_v"] = din("sb_w_v", [D, D])
    if has_b:
        Wd["b_w_q"] = din("b_w_q", [2, D, D]); Wd["b_w_o"] = din("b_w_o", [2, D, D])
    if need_kv_proj or has_b:
        kvkind = "ExternalInput" if kv_in else ("ExternalOutput" if kv_out else "Internal")
        kTd = nc.dram_tensor("kTd", [8, 128, S], BF16, kind=kvkind).ap()
        Vd = nc.dram_tensor("Vd", [S, D], BF16, kind=kvkind).ap()

    es = ExitStack()
    with es:
        def sb(name, shape, dt):
            return es.enter_context(nc.sbuf_tensor(name, list(shape), dt))

        ident = sb("ident", [128, 128], BF16)
        negtri = sb("negtri", [128, 128], BF16)
        negones = sb("negones", [128, 128], BF16)
        maskd = sb("maskd", [128, 4, 512], BF16)
        wsT = sb("wsT", [128, 2, 8, 128], BF16)
        bsb = sb("bsb", [128, 2, 8, 128], F32)
        x = sb("x", [128, 4, D], F32)
        xb = sb("xb", [128, 4, D], BF16)
        xT = sb("xT", [128, 8, TT], BF16)
        gbc = sb("gbc", [128, 2, 2, D], F32)
        uT = sb("uT", [128, 8, TT], BF16)
        vf = sb("vf", [128, 4, D], F32)
        vb = sb("vb", [128, 4, D], BF16)
        R1 = sb("R1", [128, 32 * TT], BF16)
        ebuf = sb("ebuf", [128, 2, 2, TT], F32)
        lbuf = sb("lbuf", [128, 2, 2, TT], BF16)
        lsum = sb("lsum", [128, 3, 2, TT], BF16)
        abuf = sb("abuf", [128, 2, 2, TT], BF16)
        qT = sb("qT", [128, 8, TT], BF16)
        oT = sb("oT", [128, 8, TT], BF16)
        wbuf = sb("wbuf", [128, NWBUF, 8, TT], BF16)
        st = sb("st", [128, 4, 12], F32)
        mv = sb("mv", [128, 4, 2], F32)
        lnv = sb("lnv", [128, 4], F32)
        rstd = sb("rstd", [128, 4], F32)
        nmr = sb("nmr", [128, 4], F32)
        PS = es.enter_context(nc.psum_tensor("PS", [128, 8, 512], F32))
        pT = PS[:, 7, :].bitcast(BF16).rearrange("p (a t) -> p a t", a=2)

        P = Prog(nc, es)
        cf = ebuf[:, :, :, :].rearrange("p a b t -> p (a b) t")
        wsf = vf[:, 0, :].rearrange("p (g t) -> p g t", t=128)
        wsb = vb[:, 0, :].rearrange("p (g t) -> p g t", t=128)

        def hT(ffc):
            return R1[:, ffc * TT:(ffc + 1) * TT]

        def kTb(slot):
            return R1[:, slot * 4096:(slot + 1) * 4096]

        def Vb(slot, kb):
            o = 8192 + slot * 4096 + kb * 128
            return R1[:, o:o + 128]

        def r1keys_h(ffc):
            return ["R1_%d" % ffc]

        def r1keys_kT(slot):
            return ["R1_%d" % i for i in range(slot * 8, slot * 8 + 8)]

        def r1keys_V(slot):
            return ["R1_%d" % i for i in range(16 + slot * 8, 16 + slot * 8 + 8)]

        P.op("pool", lambda e: e.memset(cf[:, 0, 0:128], 0.0), writes=["ebuf0"])
        P.op("pool", lambda e: e.affine_select(out=cf[:, 0, 0:128], in_=cf[:, 0, 0:128], pattern=[[-1, 128]],
                                                compare_op=ALU.not_equal, fill=1.0, base=0, channel_multiplier=1),
             reads=["ebuf0"], writes=["ebuf0"])
        P.op("dve", lambda e: e.tensor_copy(ident[:], cf[:, 0, 0:128]), reads=["ebuf0"], writes=["ident"])
        if has_b:
            P.op("pool", lambda e: e.memset(cf[:, 1, 0:128], -1.0), reads=["ebuf0"], writes=["ebuf0"])
            P.op("pool", lambda e: e.affine_select(out=cf[:, 1, 0:128], in_=cf[:, 1, 0:128], pattern=[[-1, 128]],
                                                    compare_op=ALU.is_ge, fill=0.0, base=0, channel_multiplier=1),
                 reads=["ebuf0"], writes=["ebuf0"])
            P.op("dve", lambda e: e.tensor_copy(negtri[:], cf[:, 1, 0:128]), reads=["ebuf0"], writes=["negtri"])
            P.op("dve", lambda e: e.memset(negones[:], -1.0), writes=["negones"])
            P.op("pool", lambda e: e.memset(cf[:, :, :], NEG_BIG), reads=["ebuf0", "ebuf1"], writes=["ebuf0", "ebuf1"])
            P.op("pool", lambda e: e.affine_select(out=cf[:, :, :], in_=cf[:, :, :], pattern=[[128, 4], [-1, 512]],
                                                    compare_op=ALU.is_ge, fill=0.0, base=0, channel_multiplier=1),
                 reads=["ebuf0", "ebuf1"], writes=["ebuf0", "ebuf1"])
            P.op("dve", lambda e: e.tensor_copy(maskd[:, :, :], cf[:, :, :]), reads=["ebuf0", "ebuf1"], writes=["maskd"])
        if has_a:
            for l in range(2):
                P.dma("sp", lambda e, l=l: e.dma_start(out=wsf[:, :, :], in_=Wd["a_w_s"][l].rearrange("g t s -> t g s")),
                      "cst_ws%d" % l, writes=["bB0"])
                P.op("dve", lambda e: e.tensor_copy(wsb[:, :, :], wsf[:, :, :]), reads=["bB0"], writes=["vb0"])
                for g in range(8):
                    P.op("pe", lambda e, g=g: e.transpose(pT[:, g // 4, (g % 4) * 128:(g % 4 + 1) * 128], wsb[:, g, :], ident[:]),
                         reads=["vb0", "ident"], writes=["ps7"], signal=(g == 7))
                P.op("dve", lambda e, l=l: e.tensor_copy(wsT[:, l, :, :], pT[:, :, :].rearrange("p a (q t) -> p (a q) t", t=128)), reads=["ps7"], writes=["wsT"])
                P.op("dve", lambda e, l=l: e.memset(wsT[64:128, l, :, 0:64], 0.0), reads=["wsT"], writes=["wsT"])
                P.dma("sp", lambda e, l=l: e.dma_start(out=bsb[:, l, :, :].rearrange("p g t -> p (g t)"), in_=Wd["a_b_s"][l:l + 1, :].broadcast_to([128, 1024])),
                      "cst_bs%d" % l, writes=["bsb"])

        def piece_src(p):
            kind = p[0]
            if kind == "win_u":
                return Wd["a_w_in"][p[1]], 0, p[2] * 512
            if kind == "win_v":
                return Wd["a_w_in"][p[1]], 0, 1024 + p[2] * 512
            if kind == "wout":
                return Wd["a_w_out"][p[1]], 0, p[2] * 512
            if kind == "w1":
                return Wd["ffn_w1"][p[1]], 0, p[2] * 512
            if kind == "w2":
                return Wd["ffn_w2"][p[1]], p[3] * 8, p[2] * 512
            if kind == "wk":
                return Wd["sb_w_k"], 0, p[1] * 512
            if kind == "wv":
                return Wd["sb_w_v"], 0, p[1] * 512
            if kind == "wq":
                return Wd["b_w_q"][p[1]], 0, p[2] * 512
            if kind == "wo":
                return Wd["b_w_o"][p[1]], 0, p[2] * 512
            raise KeyError(kind)

        def pieces_ffn(l):
            return [("w1", l, g) for g in range(8)] + [("w2", l, h, g) for h in range(2) for g in range(4)]

        def pieces_tile():
            out = []
            for l in layers:
                if l < N_A:
                    out += [("win_v", l, 0), ("win_v", l, 1), ("win_u", l, 0), ("win_u", l, 1), ("wout", l, 0), ("wout", l, 1)]
                    out += pieces_ffn(l)
                    if l == 1:
                        out += [("wk", 0), ("wk", 1), ("wv", 0), ("wv", 1)]
                else:
                    j = l - N_A
                    out += [("wq", j, 0), ("wq", j, 1), ("wo", j, 0), ("wo", j, 1)]
                    out += pieces_ffn(l)
            return out

        sched = []
        for J in range(NT):
            sched += pieces_tile()
        wstate = {"issued": 0, "next": 0}

        def issue_piece(i):
            p = sched[i]
            slot = i % NWBUF
            W2, r0, c0 = piece_src(p)
            src = W2[r0 * 128:(r0 + 8) * 128, c0:c0 + 512].rearrange("(k p) c -> p k c", p=128)
            P.dma("pool", lambda e, src=src, slot=slot: e.dma_start(out=wbuf[:, slot, :, :], in_=src),
                  "w%d" % slot, writes=["wbuf%d" % slot])

        def get_piece(expect, keep_prev=False):
            i = wstate["next"]
            assert sched[i] == expect, (sched[i], expect)
            released = i - (1 if keep_prev else 0)
            while (wstate["issued"] < len(sched) and wstate["issued"] < i + 1 + LOOKAHEAD
                   and wstate["issued"] - NWBUF < released):
                issue_piece(wstate["issued"])
                wstate["issued"] += 1
            assert wstate["issued"] > i
            wstate["next"] += 1
            slot = i % NWBUF
            return slot, "wbuf%d" % slot

        rot = {"i": 0}

        def rotbank():
            b = 4 + rot["i"] % 3
            rot["i"] += 1
            return b

        gbstate = {"i": 0}

        def load_gb(gname, bname, l):
            slot = gbstate["i"] % 2
            gbstate["i"] += 1
            key = "gb%d" % slot
            P.dma("sp", lambda e: e.dma_start(out=gbc[:, slot, 0, :], in_=Wd[gname][l:l + 1, :].broadcast_to([128, D])),
                  "gbsg%d" % slot, writes=[key])
            P.dma("sp", lambda e: e.dma_start(out=gbc[:, slot, 1, :], in_=Wd[bname][l:l + 1, :].broadcast_to([128, D])),
                  "gbsb%d" % slot, writes=[key + "b"])
            return slot

        XT_KEYS = ["xTs%d" % s_ for s_ in range(4)]
        trc = {"i": 0}

        def tr_sub(s):
            b = rotbank()
            tb = PS[:, b, :].bitcast(BF16).rearrange("p (k t) -> p k t", t=128)
            for k in range(8):
                P.op("pe", lambda e, s=s, k=k, tb=tb: e.transpose(tb[:, k, :], xb[:, s, k * 128:(k + 1) * 128], ident[:]),
                     reads=["xb%d" % s, "ident"], writes=["ps%d" % b], signal=(k == 7))
            P.op("act", lambda e, s=s, tb=tb: e.copy(xT[:, :, s * 128:(s + 1) * 128], tb[:, :, :]), reads=["ps%d" % b], writes=["xTs%d" % s])

        class LNPipe:
            def __init__(self, stages):
                self.stages = stages
                self.i = 0

            def step(self):
                i = self.i
                for k_, stage in enumerate(self.stages):
                    s_ = i - k_
                    if 0 <= s_ < 4:
                        stage(s_)
                self.i += 1

            def finish(self):
                while self.i < 4 + len(self.stages) - 1:
                    self.step()

        def layer_norm(buf, bkey, gslot, out_bf, okey, with_f32=True, do_xT=True, need_bf=True, tail_stage=None, early_h0=True):
            gk, bk = "gb%d" % gslot, "gb%db" % gslot

            def stats(s):
                if not early_h0:
                    P.op("dve", lambda e: e.bn_stats(st[:, s, 0:6], buf[:, s, 0:512]), reads=[bkey % s], writes=["st%d" % s])
                P.op("dve", lambda e: e.bn_stats(st[:, s, 6:12], buf[:, s, 512:1024]), reads=[bkey % s], writes=["st%d" % s])
                P.op("dve", lambda e: e.bn_aggr(mv[:, s, :], st[:, s, :]), reads=["st%d" % s], writes=["mv%d" % s])

            def rs(s):
                P.op("act", lambda e: e.activation(out=lnv[:, s:s + 1], in_=mv[:, s, 1:2], func=AF.Ln, bias=LN_EPS), reads=["mv%d" % s], writes=["lnv%d" % s])
                P.op("act", lambda e: e.activation(out=rstd[:, s:s + 1], in_=lnv[:, s:s + 1], func=AF.Exp, scale=-0.5), reads=["lnv%d" % s], writes=["rstd%d" % s])

            def opA(s):
                P.op("dve", lambda e: e.scalar_tensor_tensor(out=buf[:, s, :], in0=buf[:, s, :], scalar=mv[:, s, 0:1], in1=gbc[:, gslot, 0, :],
                                                             op0=ALU.subtract, op1=ALU.mult),
                     reads=[bkey % s, "mv%d" % s, gk], writes=[bkey % s])

            def opB(s):
                if with_f32:
                    P.op("dve", lambda e: e.scalar_tensor_tensor(out=buf[:, s, :], in0=buf[:, s, :], scalar=rstd[:, s:s + 1], in1=gbc[:, gslot, 1, :],
                                                                 op0=ALU.mult, op1=ALU.add),
                         reads=[bkey % s, "rstd%d" % s, bk], writes=[bkey % s])
                else:
                    P.op("dve", lambda e: e.scalar_tensor_tensor(out=out_bf[:, s, :], in0=buf[:, s, :], scalar=rstd[:, s:s + 1], in1=gbc[:, gslot, 1, :],
                                                                 op0=ALU.mult, op1=ALU.add),
                         reads=[bkey % s, "rstd%d" % s, bk], writes=[okey % s])

            def cpy(s):
                P.op("act", lambda e: e.copy(out_bf[:, s, :], buf[:, s, :]), reads=[bkey % s], writes=[okey % s])

            stages = [stats, rs, opA, opB]
            if with_f32 and need_bf:
                stages.append(cpy)
                if do_xT:
                    stages.append(tr_sub)
            if tail_stage is not None:
                stages.append(tail_stage)
            return LNPipe(stages)

        def fm_piece(pi, slot, wkey, srcT, evac_j, split):
            if split:
                banks = [0, 1, 2, 3] if pi % 2 == 0 else [4, 5, 6, 7]
                for hf in range(2):
                    rk = ["xTs%d" % (2 * hf), "xTs%d" % (2 * hf + 1)]
                    for j in range(4):
                        b = banks[j]
                        for k in range(8):
                            P.op("pe", lambda e, j=j, k=k, b=b, hf=hf: e.matmul(PS[:, b, hf * 256:(hf + 1) * 256], wbuf[:, slot, k, j * 128:(j + 1) * 128],
                                                                            srcT[:, k, hf * 256:(hf + 1) * 256], start=(k == 0), stop=(k == 7)),
                                 reads=[wkey] + rk, writes=["ps%d" % b], signal=(k == 7))
                        if hf == 1:
                            evac_j(j, b)
            else:
                for j in range(4):
                    b = rotbank()
                    for k in range(8):
                        P.op("pe", lambda e, j=j, k=k, b=b: e.matmul(PS[:, b, :], wbuf[:, slot, k, j * 128:(j + 1) * 128], srcT[:, k, :], start=(k == 0), stop=(k == 7)),
                             reads=[wkey] + XT_KEYS, writes=["ps%d" % b], signal=(k == 7))
                    evac_j(j, b)

        def gemm_fm(pieces, srcT, src_keys, evac, nsplit=0):
            for pi, pc in enumerate(pieces):
                slot, wkey = get_piece(pc)
                fm_piece(pi, slot, wkey, srcT, (lambda j, b, pi=pi: evac(pi * 4 + j, b)), split=(pi < nsplit))

        def gemm_tm(pieces, srcT, src_keys, evac, after_s=None):
            slots = [get_piece(pieces[0]), get_piece(pieces[1], keep_prev=True)]
            for s in range(4):
                for half in range(2):
                    slot, wkey = slots[half]
                    b = (2 * s + half) % 4
                    for k in range(8):
                        P.op("pe", lambda e, slot=slot, s=s, k=k, b=b: e.matmul(PS[:, b, :], srcT[:, k, s * 128:(s + 1) * 128], wbuf[:, slot, k, :], start=(k == 0), stop=(k == 7)),
                             reads=[wkey] + src_keys(s, k), writes=["ps%d" % b], signal=(k == 7))
                    evac(s, half, b)
                if after_s is not None:
                    after_s(s)

        cur = {"x": x, "xk": "bA%d", "v": vf, "vk": "bB%d"}

        def evac_resid(s, half, b):
            xx, xk = cur["x"], cur["xk"]
            P.op("dve", lambda e: e.scalar_tensor_tensor(out=xx[:, s, half * 512:(half + 1) * 512], in0=xx[:, s, half * 512:(half + 1) * 512], scalar=ALPHA,
                                                         in1=PS[:, b, :], op0=ALU.mult, op1=ALU.add),
                 reads=[xk % s, "ps%d" % b], writes=[xk % s])
            if half == 0:
                P.op("dve", lambda e: e.bn_stats(st[:, s, 0:6], xx[:, s, 0:512]), reads=[xk % s], writes=["st%d" % s])

        def ffn(l, need_xT=True, mid_hook=None, tail_stage=None):
            gslot = load_gb("ffn_ln_g", "ffn_ln_b", l)
            def evac_h(g, j, b):
                ffc = g * 4 + j
                es_ = ffc % 2
                P.op("act", lambda e: e.activation(out=ebuf[:, es_, 0, :], in_=PS[:, b, :], func=AF.Relu),
                     reads=["ps%d" % b], writes=["ebuf%d" % es_])
                P.op("dve", lambda e: e.tensor_tensor(out=hT(ffc), in0=ebuf[:, es_, 0, :], in1=ebuf[:, es_, 0, :], op=ALU.mult),
                     reads=["ebuf%d" % es_], writes=r1keys_h(ffc))
            for g in range(8):
                slot, wkey = get_piece(("w1", l, g))
                fm_piece(g, slot, wkey, xT, (lambda j, b, g=g: evac_h(g, j, b)), split=(g < NSPLIT))
            if mid_hook is not None:
                mid_hook()
            lnp = layer_norm(cur["x"], cur["xk"], gslot, xb, "xb%d", do_xT=need_xT, need_bf=need_xT, tail_stage=tail_stage)
            for half in range(2):
                for g8 in range(4):
                    slot, wkey = get_piece(("w2", l, half, g8))
                    for s in range(4):
                        for c in range(8):
                            ffc = g8 * 8 + c
                            yb = s + 4 * half
                            P.op("pe", lambda e, slot=slot, s=s, c=c, ffc=ffc, g8=g8, yb=yb: e.matmul(PS[:, yb, :], hT(ffc)[:, s * 128:(s + 1) * 128], wbuf[:, slot, c, :],
                                                                                                 start=(g8 == 0 and c == 0), stop=(g8 == 3 and c == 7)),
                                 reads=[wkey] + r1keys_h(ffc), writes=["ps%d" % yb], signal=(c == 7))
                        if g8 == 3:
                            evac_resid(s, half, s + 4 * half)
                            if half == 1:
                                lnp.step()
            lnp.finish()

        def gmlp(l):
            gslot_v = load_gb("a_ln_g", "a_ln_b", l)
            gslot_m = load_gb("mix_ln_g", "mix_ln_b", l)

            vv, vk = cur["v"], cur["vk"]

            def evac_v(s, half, b):
                P.op("act", lambda e: e.activation(out=vv[:, s, half * 512:(half + 1) * 512], in_=PS[:, b, :], func=AF.Gelu_apprx_tanh),
                     reads=["ps%d" % b], writes=[vk % s])
                if half == 0:
                    P.op("dve", lambda e: e.bn_stats(st[:, s, 0:6], vv[:, s, 0:512]), reads=[vk % s], writes=["st%d" % s])
            lnv_ = layer_norm(vv, vk, gslot_v, vb, "vb%d", with_f32=False)
            gemm_tm([("win_v", l, 0), ("win_v", l, 1)], xT, lambda s_, k_: ["xTs%d" % s_], evac_v, after_s=lambda s_: lnv_.step())

            def evac_u(oc, b):
                P.op("act", lambda e: e.activation(out=uT[:, oc, :], in_=PS[:, b, :], func=AF.Gelu_apprx_tanh),
                     reads=["ps%d" % b], writes=["uT%d" % oc])
            lnv_.finish()
            gemm_fm([("win_u", l, 0), ("win_u", l, 1)], xT, XT_KEYS, evac_u)
            for g in range(8):
                b = g
                for s in range(4):
                    P.op("pe", lambda e, g=g, s=s, b=b: e.matmul(PS[:, b, s * 128:(s + 1) * 128], vb[:, s, g * 128:(g + 1) * 128], wsT[:, l, g, :], start=True, stop=True),
                         reads=["vb%d" % s, "wsT"], writes=["ps%d" % b], signal=(s == 3))
                es_ = g % 2
                P.op("dve", lambda e, g=g, b=b, es_=es_: e.tensor_tensor(out=ebuf[:, es_, 0, :].rearrange("p (s t) -> p s t", t=128),
                                                                       in0=PS[:, b, :].rearrange("p (s t) -> p s t", t=128),
                                                                       in1=bsb[:, l, g:g + 1, :].broadcast_to([128, 4, 128]), op=ALU.add),
                     reads=["ps%d" % b, "bsb"], writes=["ebuf%d" % es_])
                P.op(GATE_ENG, lambda e, g=g, es_=es_: e.tensor_tensor(out=uT[:, g, :], in0=ebuf[:, es_, 0, :], in1=uT[:, g, :], op=ALU.mult),
                     reads=["ebuf%d" % es_, "uT%d" % g], writes=["uT%d" % g])
            lnm_ = layer_norm(cur["x"], cur["xk"], gslot_m, xb, "xb%d")
            gemm_tm([("wout", l, 0), ("wout", l, 1)], uT, lambda s_, k_: ["uT%d" % k_], evac_resid, after_s=lambda s_: lnm_.step())
            lnm_.finish()

        def kv_proj(J):
            def evac_k(oc, b):
                P.op("act", lambda e: e.copy(uT[:, oc, :], PS[:, b, :]), reads=["ps%d" % b], writes=["uT%d" % oc])
            gemm_fm([("wk", 0), ("wk", 1)], xT, XT_KEYS, evac_k, nsplit=NSPLIT)

            def evac_vv(s, half, b):
                P.op("dve", lambda e: e.tensor_copy(vb[:, s, half * 512:(half + 1) * 512], PS[:, b, :]), reads=["ps%d" % b], writes=["vb%d" % s])
            gemm_tm([("wv", 0), ("wv", 1)], xT, lambda s_, k_: ["xTs%d" % s_], evac_vv)
            P.dma("sp", lambda e: e.dma_start(out=kTd.rearrange("c p s -> p c s")[:, :, J * TT:(J + 1) * TT], in_=uT[:, :, :]),
                  "kst", reads=["uT%d" % k for k in range(8)], writes=["kd%d" % J])
            P.dma("sp", lambda e: e.dma_start(out=Vd[J * TT:(J + 1) * TT, :].rearrange("(s p) d -> p s d", p=128), in_=vb[:, :, :]),
                  "vst", reads=["vb%d" % s for s in range(4)], writes=["vd%d" % J])

        def load_kv_chunk(J, c):
            slot = c % 2
            n = (J + 1) * TT
            nkb = 4 * (J + 1)
            P.dma("sp", lambda e: e.dma_start(out=kTb(slot)[:, 0:n], in_=kTd[c][:, 0:n]),
                  "kl%d" % slot, reads=["kd%d" % t for t in range(J + 1)], writes=r1keys_kT(slot))
            vdst = R1[:, 8192 + slot * 4096: 8192 + slot * 4096 + nkb * 128].rearrange("p (b d) -> p b d", d=128)
            P.dma("sp", lambda e: e.dma_start(out=vdst, in_=Vd[0:n, c * 128:(c + 1) * 128].rearrange("(b p) d -> p b d", p=128)),
                  "vl%d" % slot, reads=["vd%d" % t for t in range(J + 1)], writes=r1keys_V(slot))

        def attention(J):
            nkb = 4 * (J + 1)
            U = nkb
            G = 8 * U

            def info(g):
                c, u = divmod(g, U)
                kb = nkb - 1 - u
                i = kb - 4 * J
                return c, u, kb, i, (i * 128 if i > 0 else 0)

            def Zop(g):
                c, u, kb, i, c0 = info(g)
                slot = c % 2
                zb = 2 * (g % 3)
                for h in range(2):
                    P.op("pe", lambda e, h=h: e.matmul(PS[:, zb + h, c0:512], kTb(slot)[h * 64:(h + 1) * 64, kb * 128:(kb + 1) * 128], qT[h * 64:(h + 1) * 64, c, c0:512],
                                                       start=True, stop=(i < 0)),
                         reads=r1keys_kT(slot) + ["qT%d" % c], writes=["ps%d" % (zb + h)], signal=(h == 1 and i < 0))
                if i >= 0:
                    for h in range(2):
                        P.op("pe", lambda e, h=h: e.matmul(PS[:, zb + h, c0:512], ident[:], maskd[:, i, c0:512], start=False, stop=True),
                             reads=["ident", "maskd"], writes=["ps%d" % (zb + h)], signal=(h == 1))

            def TRIop(g):
                c, u, kb, i, c0 = info(g)
                zb = 2 * (g % 3)
                for h in range(2):
                    P.op("pe", lambda e, h=h: e.matmul(PS[:, zb + h, c0:512], negtri[:], lbuf[:, g % 2, h, c0:512], start=False, stop=True, skip_group_check=True),
                         reads=["negtri", "lbuf%d" % (g % 2)], writes=["ps%d" % (zb + h)], signal=(u == 0 and h == 1))
                if u > 0:
                    pc0 = info(g - 1)[4]
                    for h in range(2):
                        P.op("pe", lambda e, h=h: e.matmul(PS[:, zb + h, pc0:512], negones[:], lsum[:, (g - 1) % 3, h, pc0:512], start=False, stop=True, skip_group_check=True),
                             reads=["negones", "lsum%d" % ((g - 1) % 3)], writes=["ps%d" % (zb + h)], signal=(h == 1))

            def AVop(g):
                c, u, kb, i, c0 = info(g)
                slot = c % 2
                ob = 6 + (c % 2)
                for h in range(2):
                    P.op("pe", lambda e, h=h: e.matmul(PS[h * 64:(h + 1) * 64, ob, c0:512], Vb(slot, kb)[:, h * 64:(h + 1) * 64], abuf[:, g % 2, h, c0:512],
                                                       start=(u == 0), stop=(u == U - 1), skip_group_check=True),
                         reads=r1keys_V(slot) + ["abuf%d" % (g % 2)], writes=["ps%d" % ob], signal=(h == 1))
                if u == U - 1:
                    P.op("dve", lambda e: e.tensor_copy(oT[:, c, :], PS[:, ob, :]), reads=["ps%d" % ob], writes=["oT%d" % c])
                    if c + 2 < 8:
                        load_kv_chunk(J, c + 2)

            def E1L(g):
                c, u, kb, i, c0 = info(g)
                zb = 2 * (g % 3)
                P.op("act", lambda e: e.activation(out=ebuf[:, g % 2, :, c0:512], in_=PS[:, zb:zb + 2, c0:512], func=AF.Exp),
                     reads=["ps%d" % zb, "ps%d" % (zb + 1)], writes=["ebuf%d" % (g % 2)])
                P.op("act", lambda e: e.activation(out=lbuf[:, g % 2, :, c0:512], in_=ebuf[:, g % 2, :, c0:512], func=AF.Ln, bias=1.0),
                     reads=["ebuf%d" % (g % 2)], writes=["lbuf%d" % (g % 2)])

            def E2(g):
                c, u, kb, i, c0 = info(g)
                zb = 2 * (g % 3)
                P.op("act", lambda e: e.activation(out=abuf[:, g % 2, :, c0:512], in_=PS[:, zb:zb + 2, c0:512], func=AF.Exp),
                     reads=["ps%d" % zb, "ps%d" % (zb + 1)], writes=["abuf%d" % (g % 2)])

            def LS(g):
                c, u, kb, i, c0 = info(g)
                if u + 1 >= U:
                    return
                cur, prv = g % 3, (g - 1) % 3
                if u == 0:
                    P.op("dve", lambda e: e.tensor_copy(lsum[:, cur, :, c0:512], lbuf[:, g % 2, :, c0:512]), reads=["lbuf%d" % (g % 2)], writes=["lsum%d" % cur])
                else:
                    pc0 = info(g - 1)[4]
                    P.op("dve", lambda e: e.tensor_tensor(out=lsum[:, cur, :, pc0:512], in0=lsum[:, prv, :, pc0:512], in1=lbuf[:, g % 2, :, pc0:512], op=ALU.add),
                         reads=["lsum%d" % prv, "lbuf%d" % (g % 2)], writes=["lsum%d" % cur])
                    if pc0 > c0:
                        P.op("dve", lambda e: e.tensor_copy(lsum[:, cur, :, c0:pc0], lbuf[:, g % 2, :, c0:pc0]), reads=["lbuf%d" % (g % 2)], writes=["lsum%d" % cur])

            Zop(0)
            for step in range(G + 2):
                if 0 <= step - 1 < G:
                    TRIop(step - 1)
                if step + 1 < G:
                    Zop(step + 1)
                if 0 <= step - 2 < G:
                    AVop(step - 2)
                if step < G:
                    E1L(step)
                if 0 <= step - 1 < G:
                    E2(step - 1)
                if step < G:
                    LS(step)

        def sb_layer(l, J):
            j = l - N_A
            gslot_m = load_gb("mix_ln_g", "mix_ln_b", l)

            def evac_q(oc, b):
                P.op("act", lambda e: e.activation(out=qT[:, oc, :], in_=PS[:, b, :], func=AF.Identity, scale=0.125),
                     reads=["ps%d" % b], writes=["qT%d" % oc])
            load_kv_chunk(J, 0)
            load_kv_chunk(J, 1)
            gemm_fm([("wq", j, 0), ("wq", j, 1)], xT, XT_KEYS, evac_q, nsplit=NSPLIT)
            attention(J)
            lnm_ = layer_norm(cur["x"], cur["xk"], gslot_m, xb, "xb%d")
            gemm_tm([("wo", j, 0), ("wo", j, 1)], oT, lambda s_, k_: ["oT%d" % k_], evac_resid, after_s=lambda s_: lnm_.step())
            lnm_.finish()

        bufs = [(x, "bA%d"), (vf, "bB%d")]

        def load_x(J):
            dst, kf = bufs[J % 2]
            P.dma("sp", lambda e: e.dma_start(out=dst[:, :, :], in_=xin[J * TT:(J + 1) * TT, :].rearrange("(s p) d -> p s d", p=128)),
                  "xld", writes=[kf % s for s in range(4)])

        def prologue(J):
            src, kf = bufs[J % 2]
            for s in range(4):
                P.op("act", lambda e, s=s: e.copy(xb[:, s, :], src[:, s, :]), reads=[kf % s], writes=["xb%d" % s])
                tr_sub(s)

        a_layers = [l for l in layers if l < N_A]
        load_x(0)
        prologue(0)
        for J in range(NT):
            cur["x"], cur["xk"] = bufs[J % 2]
            cur["v"], cur["vk"] = bufs[(J + 1) % 2]
            nxt = (J + 1 < NT)
            if nxt and not a_layers:
                load_x(J + 1)
            hooked = False
            for l in layers:
                last = (l == layers[-1])
                hook = None
                if last and nxt and l != 1:
                    hook = (lambda J=J: prologue(J + 1))
                    hooked = True
                tail = None
                if last:
                    xs, xkf = cur["x"], cur["xk"]

                    def tail(s_, J=J, xs=xs, xkf=xkf):
                        P.dma("sp", lambda e: e.dma_start(out=xout[J * TT + s_ * 128:J * TT + (s_ + 1) * 128, :], in_=xs[:, s_, :]),
                              "xst%d" % s_, reads=[xkf % s_])
                if l < N_A:
                    gmlp(l)
                    if nxt and l == a_layers[-1]:
                        load_x(J + 1)
                    ffn(l, need_xT=(l == 1 or not last), mid_hook=hook, tail_stage=tail)
                    if l == 1:
                        kv_proj(J)
                else:
                    sb_layer(l, J)
                    ffn(l, need_xT=not last, mid_hook=hook, tail_stage=tail)
            if nxt and not hooked:
                prologue(J + 1)
        for s_ in range(4):
            P.wait_tok("sp", Tok("xst%d" % s_, "dma", P.dcnt["xst%d" % s_]))
        if need_kv_proj and kv_out:
            P.wait_tok("sp", Tok("kst", "dma", P.dcnt["kst"]))
            P.wait_tok("sp", Tok("vst", "dma", P.dcnt["vst"]))
        block = es.enter_context(nc.Block())
        P.emit(block)
    return nc


_WNAMES_ALL = ["a_w_in", "a_ln_g", "a_ln_b", "a_w_s", "a_b_s", "a_w_out", "sb_w_k", "sb_w_v", "b_w_q", "b_w_o",
               "mix_ln_g", "mix_ln_b", "ffn_ln_g", "ffn_ln_b", "ffn_w1", "ffn_w2"]


def _wmap(inputs, layers):
    layers = tuple(layers)
    has_a = any(l < N_A for l in layers)
    has_b = any(l >= N_A for l in layers)
    names = ["mix_ln_g", "mix_ln_b", "ffn_ln_g", "ffn_ln_b", "ffn_w1", "ffn_w2"]
    if has_a:
        names += ["a_w_in", "a_ln_g", "a_ln_b", "a_w_s", "a_b_s", "a_w_out"]
    if 1 in layers:
        names += ["sb_w_k", "sb_w_v"]
    if has_b:
        names += ["b_w_q", "b_w_o"]
    m = {}
    for n in names:
        a = np.ascontiguousarray(np.asarray(inputs[n], dtype=np.float32))
        if n == "a_b_s":
            a = a.reshape(2, 8 * 128)
        m[n] = a
    return m


FUSED = True


def kernel(**inputs):
    x = np.ascontiguousarray(np.asarray(inputs["x"], dtype=np.float32))
    B, S, _ = x.shape
    NT = S // TT
    cores = list(range(B))
    if FUSED:
        nc = build(NT, (0, 1, 2, 3))
        wm = _wmap(inputs, (0, 1, 2, 3))
        in_maps = [dict(wm, xin=x[b]) for b in range(B)]
        res = run_bass_kernel_spmd(nc, in_maps, core_ids=cores)
        return np.stack([np.asarray(r["xout"]) for r in res.results], axis=0).astype(np.float32)
    cur = [x[b] for b in range(B)]
    kv = None
    for l in range(4):
        nc = build(NT, (l,), kv_in=(l >= 2), kv_out=(l == 1))
        wm = _wmap(inputs, (l,))
        in_maps = []
        for b in range(B):
            m = dict(wm, xin=cur[b])
            if l >= 2:
                m["kTd"] = kv[b][0]
                m["Vd"] = kv[b][1]
            in_maps.append(m)
        res = run_bass_kernel_spmd(nc, in_maps, core_ids=cores)
        cur = [np.asarray(r["xout"]) for r in res.results]
        if l == 1:
            kv = [(np.asarray(r["kTd"]), np.asarray(r["Vd"])) for r in res.results]
    return np.stack(cur, axis=0).astype(np.float32)
```
